# Optimizing a Trainium2 kernel written in Bass

```python
import math
import jax, jax.numpy as jnp
from jax import lax
import numpy as np

D_MODEL = 1024
BATCH = 2
SEQ = 16384
DEPTH = 4

N_EVEN = (DEPTH + 1) // 2
N_ODD = DEPTH // 2
MIX_WIDTH = D_MODEL
HALF = MIX_WIDTH // 2

ML_HEADS = 4
ML_HD = HALF // ML_HEADS
ML_CHUNK = 64
FOX_HEADS = 8
FOX_HD = HALF // FOX_HEADS
FOX_QBLOCK = 128
SSM_HEADDIM = 64
SSM_HEADS = HALF // SSM_HEADDIM
SSM_GROUPS = 2
SSM_HPG = SSM_HEADS // SSM_GROUPS
SSM_STATE = 128
SSM_CONV = 4
SSM_CHUNK = 128
SSM_CONV_CH = HALF + 2 * SSM_GROUPS * SSM_STATE
POOL_WINDOWS = (2, 4, 8, 16)
POOL_GROUPS = 4
POOL_GW = HALF // POOL_GROUPS
D_FF = 256 * ((8 * D_MODEL // 3 + 255) // 256)
FFN_CONV = 3

EVEN_IN = 4 * HALF + 2 * ML_HEADS + 3 * HALF + FOX_HEADS
ODD_IN = HALF + SSM_CONV_CH + SSM_HEADS + HALF
LN_EPS = 1e-5
ALPHA = (2.0 * DEPTH) ** 0.25
BETA = (8.0 * DEPTH) ** -0.25

kernel_name = 'hybrid_mlstm_fox_mamba2_pool_deepnorm'


def split_sizes(x, sizes):
    offs = []
    acc = 0
    for s in sizes[:-1]:
        acc += s
        offs.append(acc)
    return jnp.split(x, offs, axis=-1)


def layer_norm(x, g, b):
    xf = x.astype(jnp.float32)
    mu = xf.mean(-1, keepdims=True)
    var = jnp.square(xf - mu).mean(-1, keepdims=True)
    return ((xf - mu) * lax.rsqrt(var + LN_EPS) * g.astype(jnp.float32) + b.astype(jnp.float32)).astype(x.dtype)


def rms_norm(x, g):
    xf = x.astype(jnp.float32)
    return xf * lax.rsqrt(jnp.square(xf).mean(-1, keepdims=True) + LN_EPS) * g.astype(jnp.float32)


def causal_dwconv(x, w, b):
    k, c = w.shape
    y = lax.conv_general_dilated(x, w[:, None, :].astype(x.dtype), window_strides=(1,), padding=[(k - 1, 0)], dimension_numbers=('NWC', 'WIO', 'NWC'), feature_group_count=c)
    return y + b.astype(x.dtype)


def mlstm_chunkwise(q, k, v, i_pre, f_pre):
    b, h, s, d = q.shape
    L = ML_CHUNK
    nc = s // L
    q = q.astype(jnp.float32) * (d ** -0.5)
    k = k.astype(jnp.float32)
    v = v.astype(jnp.float32)
    ig = i_pre.astype(jnp.float32)
    logf = jax.nn.log_sigmoid(f_pre.astype(jnp.float32))

    def to_chunks(t):
        return jnp.moveaxis(t.reshape(b, h, nc, L, *t.shape[3:]), 2, 0)

    causal = jnp.tril(jnp.ones((L, L), dtype=bool))

    def step(carry, inp):
        c_st, n_st, m_st = carry
        qq, kk, vv, ii, ff = inp
        bcum = jnp.cumsum(ff, axis=-1)
        a = bcum + m_st[..., None]
        dmat = jnp.where(causal, bcum[..., :, None] - bcum[..., None, :] + ii[..., None, :], -jnp.inf)
        mt = jnp.maximum(a, dmat.max(-1))
        w_intra = jnp.exp(dmat - mt[..., None])
        w_inter = jnp.exp(a - mt)
        qk = jnp.einsum('bhld,bhsd->bhls', qq, kk) * w_intra
        num = jnp.einsum('bhls,bhsd->bhld', qk, vv) + w_inter[..., None] * jnp.einsum('bhvk,bhlk->bhlv', c_st, qq)
        den = qk.sum(-1) + w_inter * jnp.einsum('bhk,bhlk->bhl', n_st, qq)
        h_out = num / jnp.maximum(jnp.abs(den), jnp.exp(-mt))[..., None]
        btot = bcum[..., -1]
        g_log = btot[..., None] - bcum + ii
        m_new = jnp.maximum(btot + m_st, g_log.max(-1))
        ws = jnp.exp(g_log - m_new[..., None])
        decay = jnp.exp(btot + m_st - m_new)
        c_new = decay[..., None, None] * c_st + jnp.einsum('bhl,bhlv,bhlk->bhvk', ws, vv, kk)
        n_new = decay[..., None] * n_st + jnp.einsum('bhl,bhlk->bhk', ws, kk)
        return (c_new, n_new, m_new), h_out

    init = (jnp.zeros((b, h, d, d), jnp.float32), jnp.zeros((b, h, d), jnp.float32), jnp.zeros((b, h), jnp.float32))
    _, hs = lax.scan(step, init, (to_chunks(q), to_chunks(k), to_chunks(v), to_chunks(ig), to_chunks(logf)))
    return jnp.moveaxis(hs, 0, 2).reshape(b, h, s, d)


def forgetting_attention(q, k, v, f_pre):
    b, h, s, d = q.shape
    F = jnp.cumsum(jax.nn.log_sigmoid(f_pre.astype(jnp.float32)), axis=-1)
    nb = s // FOX_QBLOCK
    kpos = jnp.arange(s)
    scale = d ** -0.5

    def block(i):
        start = i * FOX_QBLOCK
        qb = lax.dynamic_slice_in_dim(q, start, FOX_QBLOCK, axis=2)
        fq = lax.dynamic_slice_in_dim(F, start, FOX_QBLOCK, axis=2)
        qpos = start + jnp.arange(FOX_QBLOCK)
        logits = jnp.einsum('bhqd,bhkd->bhqk', qb, k).astype(jnp.float32) * scale + fq[..., :, None] - F[..., None, :]
        logits = jnp.where(kpos[None, :] <= qpos[:, None], logits, -jnp.inf)
        p = jax.nn.softmax(logits, axis=-1)
        return jnp.einsum('bhqk,bhkd->bhqd', p.astype(v.dtype), v)

    out = lax.map(block, jnp.arange(nb))
    return jnp.moveaxis(out, 0, 2).reshape(b, h, s, d)


def segsum(a):
    L = a.shape[-1]
    cs = jnp.cumsum(a, axis=-1)
    mask = jnp.tril(jnp.ones((L, L), dtype=bool))
    return jnp.where(mask, cs[..., :, None] - cs[..., None, :], -jnp.inf)


def ssd_chunked(x, dt_a, bm, cm):
    b, s, g, e, p = x.shape
    L = SSM_CHUNK
    nc = s // L
    x = x.reshape(b, nc, L, g, e, p)
    bm = bm.reshape(b, nc, L, g, -1)
    cm = cm.reshape(b, nc, L, g, -1)
    a = jnp.moveaxis(dt_a.reshape(b, nc, L, g, e), (1, 2), (3, 4))
    a_cs = jnp.cumsum(a, axis=-1)
    lmat = jnp.exp(segsum(a))
    cb = jnp.einsum('bclgn,bcsgn->bgcls', cm, bm)
    y_diag = jnp.einsum('bgcls,bgecls,bcsgep->bclgep', cb, lmat, x)
    decay_states = jnp.exp(a_cs[..., -1:] - a_cs)
    states = jnp.einsum('bclgn,bgecl,bclgep->bcgepn', bm, decay_states, x)
    chunk_decay = jnp.exp(a_cs[..., -1])

    def step(hst, inp):
        st, dec = inp
        return dec[..., None, None] * hst + st, hst

    h0 = jnp.zeros((b, g, e, p, states.shape[-1]), jnp.float32)
    _, prev = lax.scan(step, h0, (jnp.moveaxis(states, 1, 0), jnp.moveaxis(chunk_decay, 3, 0)))
    y_off = jnp.einsum('bclgn,cbgepn,bgecl->bclgep', cm, prev, jnp.exp(a_cs))
    return (y_diag + y_off).reshape(b, s, g, e, p)


def multiscale_pool(u, w_grp, b_grp, scale):
    b, s, c = u.shape
    uf = u.astype(jnp.float32)
    csz = jnp.concatenate([jnp.zeros((b, 1, c), jnp.float32), jnp.cumsum(uf, axis=1)], axis=1)
    t = jnp.arange(s)
    outs = []
    for gi, w in enumerate(POOL_WINDOWS):
        lo_c, hi_c = gi * POOL_GW, (gi + 1) * POOL_GW
        cg = csz[:, :, lo_c:hi_c]
        lo = jnp.maximum(t + 1 - w, 0)
        wsum = cg[:, 1:] - jnp.take(cg, lo, axis=1)
        cnt = jnp.minimum(t + 1, w).astype(jnp.float32)
        outs.append(wsum / cnt[None, :, None] - uf[:, :, lo_c:hi_c])
    pooled = jnp.stack(outs, axis=2)
    y = jnp.einsum('bsgc,gcd->bsgd', pooled, w_grp.astype(jnp.float32)).reshape(b, s, c) + b_grp.astype(jnp.float32)
    return y * scale.astype(jnp.float32)


def head_norm(h, g):
    nh, d = h.shape[1], h.shape[3]
    mu = h.mean(-1, keepdims=True)
    var = jnp.square(h - mu).mean(-1, keepdims=True)
    return (h - mu) * lax.rsqrt(var + LN_EPS) * g.astype(jnp.float32).reshape(nh, 1, d)


def even_mixer(x, w_in, b_in, ml_norm, w_out):
    b, s, _ = x.shape
    u = x @ w_in + b_in
    mq, mk, mv, mo, mi, mf, fq, fk, fv, ff = split_sizes(u, [HALF, HALF, HALF, HALF, ML_HEADS, ML_HEADS, HALF, HALF, HALF, FOX_HEADS])

    def heads(t, nh):
        return t.reshape(b, s, nh, -1).transpose(0, 2, 1, 3)

    h_ml = mlstm_chunkwise(heads(mq, ML_HEADS), heads(mk, ML_HEADS), heads(mv, ML_HEADS), mi.transpose(0, 2, 1), mf.transpose(0, 2, 1))
    h_ml = head_norm(h_ml, ml_norm).transpose(0, 2, 1, 3).reshape(b, s, HALF)
    h_ml = (jax.nn.sigmoid(mo.astype(jnp.float32)) * h_ml).astype(x.dtype)
    h_fx = forgetting_attention(heads(fq, FOX_HEADS), heads(fk, FOX_HEADS), heads(fv, FOX_HEADS), ff.transpose(0, 2, 1))
    h_fx = h_fx.transpose(0, 2, 1, 3).reshape(b, s, HALF).astype(x.dtype)
    return jnp.concatenate([h_ml, h_fx], axis=-1) @ w_out


def odd_mixer(x, w_in, conv_w, conv_b, dt_bias, a_log, d_skip, ssm_norm, pool_w, pool_b, pool_scale, w_out):
    b, s, _ = x.shape
    u = x @ w_in
    z, xbc, dt_raw, pool_in = split_sizes(u, [HALF, SSM_CONV_CH, SSM_HEADS, HALF])
    xbc = jax.nn.silu(causal_dwconv(xbc, conv_w, conv_b)).astype(jnp.float32)
    xs, bm, cm = split_sizes(xbc, [HALF, SSM_GROUPS * SSM_STATE, SSM_GROUPS * SSM_STATE])
    dt = jax.nn.softplus(dt_raw.astype(jnp.float32) + dt_bias.astype(jnp.float32))
    a = -jnp.exp(a_log.astype(jnp.float32)).reshape(SSM_GROUPS, SSM_HPG)
    xh = xs.reshape(b, s, SSM_GROUPS, SSM_HPG, SSM_HEADDIM)
    dth = dt.reshape(b, s, SSM_GROUPS, SSM_HPG)
    y = ssd_chunked(xh * dth[..., None], dth * a, bm.reshape(b, s, SSM_GROUPS, SSM_STATE), cm.reshape(b, s, SSM_GROUPS, SSM_STATE))
    y = y + d_skip.astype(jnp.float32).reshape(SSM_GROUPS, SSM_HPG, 1) * xh
    y = rms_norm(y.reshape(b, s, HALF) * jax.nn.silu(z.astype(jnp.float32)), ssm_norm).astype(x.dtype)
    p = multiscale_pool(pool_in, pool_w, pool_b, pool_scale).astype(x.dtype)
    return jnp.concatenate([y, p], axis=-1) @ w_out


def conv_ffn(x, w_up, conv_w, conv_b, w_down):
    h = causal_dwconv(x @ w_up, conv_w, conv_b)
    val, gate = jnp.split(h, 2, axis=-1)
    return (jax.nn.silu(gate) * val) @ w_down


def setup_inputs(seed: int = 0) -> dict:
    key = jax.random.key(seed)
    ks = jax.random.split(key, 40)
    E, O, L = N_EVEN, N_ODD, DEPTH
    nrm = lambda k, shape, sc: jax.random.normal(k, shape, jnp.float32) * sc
    x = nrm(ks[0], (BATCH, SEQ, D_MODEL), 1.0)
    ev_w_in = nrm(ks[1], (E, D_MODEL, EVEN_IN), D_MODEL ** -0.5)
    ev_b_in = jnp.concatenate([
        nrm(ks[2], (E, 4 * HALF), 0.02),
        nrm(ks[3], (E, ML_HEADS), 0.1),
        3.0 + 3.0 * jax.random.uniform(ks[4], (E, ML_HEADS), jnp.float32),
        nrm(ks[5], (E, 3 * HALF), 0.02),
        1.0 + 3.0 * jax.random.uniform(ks[6], (E, FOX_HEADS), jnp.float32)], axis=-1)
    ev_ml_norm = 1.0 + nrm(ks[7], (E, HALF), 0.02)
    ev_w_out = nrm(ks[8], (E, MIX_WIDTH, D_MODEL), MIX_WIDTH ** -0.5 * BETA)
    od_w_in = nrm(ks[9], (O, D_MODEL, ODD_IN), D_MODEL ** -0.5)
    od_conv_w = nrm(ks[10], (O, SSM_CONV, SSM_CONV_CH), SSM_CONV ** -0.5)
    od_conv_b = nrm(ks[11], (O, SSM_CONV_CH), 0.02)
    dt0 = jnp.exp(jax.random.uniform(ks[12], (O, SSM_HEADS), jnp.float32, math.log(1e-3), math.log(1e-1)))
    od_dt_bias = dt0 + jnp.log(-jnp.expm1(-dt0))
    od_a_log = jnp.log(jax.random.uniform(ks[13], (O, SSM_HEADS), jnp.float32, 1.0, 16.0))
    od_d_skip = 1.0 + nrm(ks[14], (O, SSM_HEADS), 0.02)
    od_ssm_norm = 1.0 + nrm(ks[15], (O, HALF), 0.02)
    od_pool_w = nrm(ks[16], (O, POOL_GROUPS, POOL_GW, POOL_GW), POOL_GW ** -0.5)
    od_pool_b = nrm(ks[17], (O, HALF), 0.02)
    od_pool_scale = 1.0 + nrm(ks[18], (O, HALF), 0.1)
    od_w_out = nrm(ks[19], (O, MIX_WIDTH, D_MODEL), MIX_WIDTH ** -0.5 * BETA)
    ffn_w_up = nrm(ks[20], (L, D_MODEL, 2 * D_FF), D_MODEL ** -0.5)
    ffn_conv_w = nrm(ks[21], (L, FFN_CONV, 2 * D_FF), FFN_CONV ** -0.5)
    ffn_conv_b = nrm(ks[22], (L, 2 * D_FF), 0.02)
    ffn_w_down = nrm(ks[23], (L, D_FF, D_MODEL), D_FF ** -0.5 * BETA)
    ln1_g = 1.0 + nrm(ks[24], (L, D_MODEL), 0.02)
    ln1_b = nrm(ks[25], (L, D_MODEL), 0.02)
    ln2_g = 1.0 + nrm(ks[26], (L, D_MODEL), 0.02)
    ln2_b = nrm(ks[27], (L, D_MODEL), 0.02)
    return {'x': x, 'ev_w_in': ev_w_in, 'ev_b_in': ev_b_in, 'ev_ml_norm': ev_ml_norm, 'ev_w_out': ev_w_out,
            'od_w_in': od_w_in, 'od_conv_w': od_conv_w, 'od_conv_b': od_conv_b, 'od_dt_bias': od_dt_bias,
            'od_a_log': od_a_log, 'od_d_skip': od_d_skip, 'od_ssm_norm': od_ssm_norm, 'od_pool_w': od_pool_w,
            'od_pool_b': od_pool_b, 'od_pool_scale': od_pool_scale, 'od_w_out': od_w_out,
            'ffn_w_up': ffn_w_up, 'ffn_conv_w': ffn_conv_w, 'ffn_conv_b': ffn_conv_b, 'ffn_w_down': ffn_w_down,
            'ln1_g': ln1_g, 'ln1_b': ln1_b, 'ln2_g': ln2_g, 'ln2_b': ln2_b}


def reference(x, ev_w_in, ev_b_in, ev_ml_norm, ev_w_out, od_w_in, od_conv_w, od_conv_b, od_dt_bias, od_a_log, od_d_skip, od_ssm_norm, od_pool_w, od_pool_b, od_pool_scale, od_w_out, ffn_w_up, ffn_conv_w, ffn_conv_b, ffn_w_down, ln1_g, ln1_b, ln2_g, ln2_b):
    for layer in range(DEPTH):
        j = layer // 2
        if layer % 2 == 0:
            y = even_mixer(x, ev_w_in[j], ev_b_in[j], ev_ml_norm[j], ev_w_out[j])
        else:
            y = odd_mixer(x, od_w_in[j], od_conv_w[j], od_conv_b[j], od_dt_bias[j], od_a_log[j], od_d_skip[j], od_ssm_norm[j], od_pool_w[j], od_pool_b[j], od_pool_scale[j], od_w_out[j])
        x = layer_norm(ALPHA * x + y, ln1_g[layer], ln1_b[layer])
        f = conv_ffn(x, ffn_w_up[layer], ffn_conv_w[layer], ffn_conv_b[layer], ffn_w_down[layer])
        x = layer_norm(ALPHA * x + f, ln2_g[layer], ln2_b[layer])
    return x
```

```python
import contextlib
import numpy as np
import concourse.bass as bass
import concourse.mybir as mybir

F32 = mybir.dt.float32
BF16 = mybir.dt.bfloat16
AF = mybir.ActivationFunctionType
ALU = mybir.AluOpType
EPOCH = 24000


class Tok:
    __slots__ = ("w", "r", "dsem", "dcnt", "name")

    def __init__(self, name=""):
        self.w = None
        self.r = []
        self.dsem = None
        self.dcnt = 0
        self.name = name


class FW:
    def __init__(self, nc, stack):
        self.nc = nc
        self.stack = stack
        self.eng = {"pe": nc.tensor, "dve": nc.vector, "act": nc.scalar, "pool": nc.gpsimd, "sp": nc.sync}
        self.sems = {}
        self.cnt = {e: 0 for e in self.eng}
        self.seen = {e: {} for e in self.eng}
        self.dma_sems = []
        self.nsem = 0
        self.n_instr = 0

    def _sem(self, key):
        if key not in self.sems:
            self.sems[key] = self.stack.enter_context(self.nc.semaphore(f"s{self.nsem}"))
            self.nsem += 1
        return self.sems[key]

    def _engkey(self, e, n):
        ep = (n - 1) // EPOCH
        return (e, ep), n - ep * EPOCH

    def tok(self, name=""):
        return Tok(name)

    def toks(self, n, name=""):
        return [Tok(f"{name}{i}") for i in range(n)]

    def _wait(self, e, deps):
        engobj = self.eng[e]
        best = {}
        for d in deps:
            if d is None:
                continue
            k, v = d
            if v > best.get(k, 0):
                best[k] = v
        for k, v in best.items():
            if v > self.seen[e].get(k, 0):
                engobj.wait_ge(self._sem(k), v)
                self.seen[e][k] = v

    def op(self, e, fn, reads=(), writes=(), acc=()):
        deps = []
        for t in reads:
            deps.append(t.w)
        for t in writes:
            deps.append(t.w)
            deps.extend(t.r)
        for t in acc:
            if t.w is not None and not (isinstance(t.w[0], tuple) and t.w[0][0] == e):
                deps.append(t.w)
            deps.extend(t.r)
        self._wait(e, deps)
        ins = fn()
        self.cnt[e] += 1
        self.n_instr += 1
        k, v = self._engkey(e, self.cnt[e])
        ins.then_inc(self._sem(k), 1)
        rec = (k, v)
        for t in reads:
            t.r.append(rec)
            if len(t.r) > 24:
                t.r = t.r[-24:] if False else self._compact(t.r)
        for t in list(writes) + list(acc):
            t.w = rec
            t.r = []
        return ins

    def _compact(self, recs):
        best = {}
        for k, v in recs:
            if v > best.get(k, 0):
                best[k] = v
        return list(best.items())

    def dma(self, q, out, in_, reads=(), writes=(), semtok=None, **kw):
        if semtok is None:
            semtok = writes[0] if writes else reads[0]
        if semtok.dsem is None:
            semtok.dsem = ("dma", len(self.dma_sems))
            self.dma_sems.append((semtok.dsem, semtok))
        deps = []
        for t in reads:
            deps.append(t.w)
        for t in writes:
            deps.append(t.w)
            deps.extend(t.r)
        if semtok.dcnt > 0:
            deps.append((semtok.dsem, semtok.dcnt))
        self._wait(q, deps)
        ins = self.eng[q].dma_start(out=out, in_=in_, **kw)
        semtok.dcnt += 16
        ins.then_inc(self._sem(semtok.dsem), 16)
        rec = (semtok.dsem, semtok.dcnt)
        self.n_instr += 1
        for t in reads:
            t.r.append(rec)
        for t in writes:
            t.w = rec
            t.r = []
        return ins

    def barrier(self, engines=("pe", "dve", "act", "pool", "sp")):
        deps = [(k, t.dcnt) for k, t in self.dma_sems if t.dcnt > 0]
        for e2 in self.eng:
            if self.cnt[e2] > 0:
                deps.append(self._engkey(e2, self.cnt[e2]))
        for e in engines:
            self._wait(e, deps)

    def finish(self, e="sp"):
        deps = [(k, t.dcnt) for k, t in self.dma_sems if t.dcnt > 0]
        self._wait(e, deps)


D = 1024
DFF = 2816
NCH = 8
NFC = 22
ALPHA = (2.0 * 4) ** 0.25
LN_EPS = 1e-5
V_LN1G, V_LN1B, V_LN2G, V_LN2B = 0, 8, 16, 24
V_CW = 32
V_CB = 32 + 132
V_HM = V_CB + 44
V_SN = V_HM + 1
NV = V_SN + 4


def build_T(NT, TW, odd):
    NTP = NT + 2
    W2 = TW + 2
    ntiles = (NT + TW - 1) // TW
    nc = bass.Bass("TRN2", target_bir_lowering=False)
    hcT = nc.dram_tensor("hcT", [D, NTP], F32, kind="ExternalInput").ap()
    xT = nc.dram_tensor("xT", [D, NTP], F32, kind="ExternalInput").ap()
    w_out = nc.dram_tensor("w_out", [D, D], F32, kind="ExternalInput").ap()
    w_up = nc.dram_tensor("w_up", [D, 2 * DFF], F32, kind="ExternalInput").ap()
    w_down = nc.dram_tensor("w_down", [DFF, D], F32, kind="ExternalInput").ap()
    vecs = nc.dram_tensor("vecs", [128, NV], F32, kind="ExternalInput").ap()
    x2T = nc.dram_tensor("x2T", [D, NT], F32, kind="ExternalOutput").ap()
    x1T = nc.dram_tensor("x1T_scr", [D, NTP], F32, kind="Internal").ap()

    hcT_v = hcT.rearrange("(k p) n -> p k n", p=128)
    xT_v = xT.rearrange("(k p) n -> p k n", p=128)
    x1T_v = x1T.rearrange("(k p) n -> p k n", p=128)
    x2T_v = x2T.rearrange("(k p) n -> p k n", p=128)

    with contextlib.ExitStack() as st:
        fw = FW(nc, st)
        sb = lambda name, shape, dt: st.enter_context(nc.sbuf_tensor(name, shape, dt))
        V = sb("vecs_sb", [128, NV], F32)
        tV = fw.tok("V")
        fw.dma("sp", V[:], vecs, writes=[tV])
        onesm = sb("onesm", [128, 128], BF16)
        tO = fw.tok("ones")
        fw.op("dve", lambda: nc.vector.memset(onesm[:], 1.0 / D), writes=[tO])
        wup = sb("wup", [128, NCH, 2 * DFF], BF16)
        wdn = sb("wdn", [128, NFC, D], BF16)
        tWup, tWdn = fw.toks(NCH, "wup"), fw.tok("wdn")
        banks = [st.enter_context(nc.psum_tensor(f"bank{i}", [128, 512], F32)) for i in range(8)]
        tB = fw.toks(8, "bank")

        def ln_block(src, ncol, gcol, bcol, tsrc, stat_banks, tmp, dst_f=None, dst_b=None, tdst_f=None, tdst_b=None,
                     col0=0):
            bm, bq = stat_banks
            rb, rq, t1, t2, m2, var, sd, rstd, nmr = tmp["rb"], tmp["rq"], tmp["t1"], tmp["t2"], tmp["m2"], tmp["var"], tmp["sd"], tmp["rstd"], tmp["nmr"]
            for m in range(NCH):
                j = m % 2
                s_ap = src[:, m, col0:col0 + ncol]
                fw.op("act", lambda: nc.scalar.activation(out=rb[j][:, :ncol], in_=s_ap, func=AF.Copy),
                      reads=[tsrc[m]], writes=[tmp["trb"][j]])
                fw.op("act", lambda: nc.scalar.activation(out=rq[j][:, :ncol], in_=s_ap, func=AF.Square),
                      reads=[tsrc[m]], writes=[tmp["trq"][j]])
                fw.op("pe", lambda: nc.tensor.matmul(banks[bm][:, :ncol], onesm[:], rb[j][:, :ncol], start=(m == 0), stop=(m == NCH - 1)),
                      reads=[tO, tmp["trb"][j]], **({"writes": [tB[bm]]} if m == 0 else {"acc": [tB[bm]]}))
                fw.op("pe", lambda: nc.tensor.matmul(banks[bq][:, :ncol], onesm[:], rq[j][:, :ncol], start=(m == 0), stop=(m == NCH - 1)),
                      reads=[tO, tmp["trq"][j]], **({"writes": [tB[bq]]} if m == 0 else {"acc": [tB[bq]]}))
            ts = tmp["tstat"]
            fw.op("act", lambda: nc.scalar.activation(out=m2[:, :ncol], in_=banks[bm][:, :ncol], func=AF.Square),
                  reads=[tB[bm]], writes=[ts[0]])
            fw.op("dve", lambda: nc.vector.tensor_tensor(out=var[:, :ncol], in0=banks[bq][:, :ncol], in1=m2[:, :ncol], op=ALU.subtract),
                  reads=[tB[bq], ts[0]], writes=[ts[1]])
            fw.op("act", lambda: nc.scalar.activation(out=sd[:, :ncol], in_=var[:, :ncol], func=AF.Sqrt, bias=tmp["eps"][:, 0:1]),
                  reads=[ts[1], tmp["teps"]], writes=[ts[2]])
            fw.op("dve", lambda: nc.vector.reciprocal(out=rstd[:, :ncol], in_=sd[:, :ncol]), reads=[ts[2]], writes=[ts[3]])
            fw.op("dve", lambda: nc.vector.scalar_tensor_tensor(out=nmr[:, :ncol], in0=banks[bm][:, :ncol], scalar=-1.0, in1=rstd[:, :ncol],
                                                                op0=ALU.mult, op1=ALU.mult),
                  reads=[tB[bm], ts[3]], writes=[ts[4]])
            for m in range(NCH):
                j = m % 2
                s_ap = src[:, m, col0:col0 + ncol]
                fw.op("pool", lambda: nc.gpsimd.tensor_tensor(out=t1[j][:, :ncol], in0=s_ap, in1=rstd[:, :ncol], op=ALU.mult),
                      reads=[tsrc[m], ts[3]], writes=[tmp["tt1"][j]])
                fw.op("dve", lambda: nc.vector.tensor_tensor(out=t2[j][:, :ncol], in0=t1[j][:, :ncol], in1=nmr[:, :ncol], op=ALU.add),
                      reads=[tmp["tt1"][j], ts[4]], writes=[tmp["tt2"][j]])
                if dst_f is not None:
                    d_ap = dst_f[:, m, 0:ncol]
                    fw.op("act", lambda: nc.scalar.activation(out=d_ap, in_=t2[j][:, :ncol], func=AF.Identity,
                                                              scale=V[:, gcol + m:gcol + m + 1], bias=V[:, bcol + m:bcol + m + 1]),
                          reads=[tmp["tt2"][j], tV], writes=[tdst_f[m]])
                if dst_b is not None:
                    d_ap = dst_b[:, m, 0:ncol]
                    fw.op("act", lambda: nc.scalar.activation(out=d_ap, in_=t2[j][:, :ncol], func=AF.Identity,
                                                              scale=V[:, gcol + m:gcol + m + 1], bias=V[:, bcol + m:bcol + m + 1]),
                          reads=[tmp["tt2"][j], tV], writes=[tdst_b[m]])

        def mk_tmp(stk):
            s2 = lambda name, shape, dt: stk.enter_context(nc.sbuf_tensor(name, shape, dt))
            tmp = {}
            tmp["rb"] = [s2(f"rb{j}", [128, W2], BF16) for j in range(2)]
            tmp["rq"] = [s2(f"rq{j}", [128, W2], BF16) for j in range(2)]
            tmp["t1"] = [s2(f"t1{j}", [128, W2], F32) for j in range(2)]
            tmp["t2"] = [s2(f"t2{j}", [128, W2], F32) for j in range(2)]
            for nm in ["m2", "var", "sd", "rstd", "nmr"]:
                tmp[nm] = s2(nm, [128, W2], F32)
            tmp["eps"] = s2("eps", [128, 1], F32)
            tmp["teps"] = fw.tok()
            fw.op("dve", lambda: nc.vector.memset(tmp["eps"][:], LN_EPS), writes=[tmp["teps"]])
            tmp["trb"], tmp["trq"], tmp["tt1"], tmp["tt2"] = fw.toks(2), fw.toks(2), fw.toks(2), fw.toks(2)
            tmp["tstat"] = fw.toks(5)
            return tmp

        tmp = mk_tmp(st)
        with contextlib.ExitStack() as p1:
            s1 = lambda name, shape, dt: p1.enter_context(nc.sbuf_tensor(name, shape, dt))
            wout = s1("wout", [128, NCH, D], BF16)
            tWout = fw.tok("wout")
            fw.dma("pool", wout[:], w_out.rearrange("(k p) n -> p k n", p=128), writes=[tWout])
            hcb = [s1(f"hcb{j}", [128, NCH, W2], BF16) for j in range(2)]
            xr = [s1(f"xr{j}", [128, NCH, W2], F32) for j in range(2)]
            thc = [fw.toks(NCH) for j in range(2)]
            thcload = fw.toks(2)
            txr = [fw.toks(NCH) for j in range(2)]
            txload = fw.toks(2)
            if odd:
                hcf = [s1(f"hcf{j}", [128, 4, W2], F32) for j in range(2)]
                thcf = fw.toks(2)
                sq = [s1(f"sq{j}", [128, W2], BF16) for j in range(2)]
                tsq = fw.toks(2)
                rs = s1("rs", [128, W2], F32)
                trs = fw.tok()
            for i in range(ntiles):
                j = i % 2
                c0 = i * TW
                wd = min(TW, NT - c0)
                ncol = wd + 2
                if not odd:
                    fw.dma("pool", hcb[j][:, :, :ncol], hcT_v[:, :, c0:c0 + ncol], writes=thc[j], semtok=thcload[j])
                else:
                    fw.dma("pool", hcb[j][:, 4:8, :ncol], hcT_v[:, 4:8, c0:c0 + ncol], writes=thc[j][4:8], semtok=thcload[j])
                    fw.dma("sp", hcf[j][:, :, :ncol], hcT_v[:, 0:4, c0:c0 + ncol], writes=[thcf[j]])
                fw.dma("sp", xr[j][:, :, :ncol], xT_v[:, :, c0:c0 + ncol], writes=txr[j], semtok=txload[j])
                if i == 0:
                    for kk in range(NCH):
                        fw.dma("pool", wup[:, kk, :], w_up[kk * 128:(kk + 1) * 128, :], writes=[tWup[kk]], max_dma_last_dim=8192)
                    fw.dma("pool", wdn[:], w_down.rearrange("(k p) n -> p k n", p=128), writes=[tWdn])
                if odd:
                    for m in range(4):
                        jj = m % 2
                        fw.op("act", lambda: nc.scalar.activation(out=sq[jj][:, :ncol], in_=hcf[j][:, m, :ncol], func=AF.Square),
                              reads=[thcf[j]], writes=[tsq[jj]])
                        fw.op("pe", lambda: nc.tensor.matmul(banks[6][:, :ncol], onesm[:], sq[jj][:, :ncol], start=(m == 0), stop=(m == 3)),
                              reads=[tO, tsq[jj]], **({"writes": [tB[6]]} if m == 0 else {"acc": [tB[6]]}))
                    fw.op("act", lambda: nc.scalar.activation(out=rs[:, :ncol], in_=banks[6][:, :ncol], func=AF.Sqrt, scale=2.0, bias=tmp["eps"][:, 0:1]),
                          reads=[tB[6], tmp["teps"]], writes=[trs])
                    fw.op("dve", lambda: nc.vector.reciprocal(out=rs[:, :ncol], in_=rs[:, :ncol]), reads=[trs], writes=[trs])
                    for m in range(4):
                        fw.op("dve", lambda: nc.vector.scalar_tensor_tensor(out=hcb[j][:, m, :ncol], in0=hcf[j][:, m, :ncol],
                                                                            scalar=V[:, V_SN + m:V_SN + m + 1], in1=rs[:, :ncol],
                                                                            op0=ALU.mult, op1=ALU.mult),
                              reads=[thcf[j], trs, tV], writes=[thc[j][m]])
                for m in range(NCH):
                    bk = m % 4
                    for k in range(NCH):
                        fw.op("pe", lambda: nc.tensor.matmul(banks[bk][:, :ncol], wout[:, k, m * 128:(m + 1) * 128], hcb[j][:, k, :ncol],
                                                             start=(k == 0), stop=(k == NCH - 1)),
                              reads=[tWout, thc[j][k]], **({"writes": [tB[bk]]} if k == 0 else {"acc": [tB[bk]]}))
                    fw.op("dve", lambda: nc.vector.scalar_tensor_tensor(out=xr[j][:, m, :ncol], in0=xr[j][:, m, :ncol], scalar=ALPHA,
                                                                        in1=banks[bk][:, :ncol], op0=ALU.mult, op1=ALU.add),
                          reads=[tB[bk]], writes=[txr[j][m]])
                ln_block(xr[j], ncol, V_LN1G, V_LN1B, txr[j], (4, 5), tmp, dst_f=xr[j], tdst_f=txr[j])
                fw.dma("sp", x1T_v[:, :, c0:c0 + ncol], xr[j][:, :, :ncol], reads=txr[j], semtok=txload[j])
        tX1 = fw.tok("x1scr")
        with contextlib.ExitStack() as p2:
            s2 = lambda name, shape, dt: p2.enter_context(nc.sbuf_tensor(name, shape, dt))
            x1b = [s2(f"x1b{j}", [128, NCH, W2], BF16) for j in range(2)]
            x1f = [s2(f"x1f{j}", [128, NCH, W2], F32) for j in range(2)]
            gT = s2("gT", [128, NFC, TW], BF16)
            tv = [s2(f"tv{j}", [128, TW], F32) for j in range(2)]
            tg = [s2(f"tg{j}", [128, TW], F32) for j in range(2)]
            sg = [s2(f"sg{j}", [128, TW], F32) for j in range(2)]
            ttv, ttg, tsg = fw.toks(2), fw.toks(2), fw.toks(2)
            tx1b = fw.toks(2)
            tx1f = [fw.toks(NCH) for j in range(2)]
            tx1fload = fw.toks(2)
            tgT = fw.toks(NFC)
            fw.barrier()
            for i in range(ntiles):
                j = i % 2
                c0 = i * TW
                wd = min(TW, NT - c0)
                ncol = wd + 2
                fw.dma("pool", x1b[j][:, :, :ncol], x1T_v[:, :, c0:c0 + ncol], writes=[tx1b[j]])
                fw.dma("sp", x1f[j][:, :, :ncol], x1T_v[:, :, c0:c0 + ncol], writes=tx1f[j], semtok=tx1fload[j])
                if i == 0:
                    fw.op("dve", lambda: nc.vector.tensor_scalar(out=x1b[j][:, :, 0:2], in0=x1b[j][:, :, 0:2], scalar1=V[:, V_HM:V_HM + 1],
                                                                 scalar2=None, op0=ALU.mult),
                          reads=[tV], writes=[tx1b[j]])
                for c in range(NFC):
                    jj = c % 2
                    bv, bg = (0, 1) if jj == 0 else (2, 3)
                    for (bk, cc) in ((bv, c), (bg, c + NFC)):
                        for k in range(NCH):
                            fw.op("pe", lambda: nc.tensor.matmul(banks[bk][:, :ncol], wup[:, k, cc * 128:(cc + 1) * 128], x1b[j][:, k, :ncol],
                                                                 start=(k == 0), stop=(k == NCH - 1)),
                                  reads=[tWup[k], tx1b[j]], **({"writes": [tB[bk]]} if k == 0 else {"acc": [tB[bk]]}))
                    for (bk, cc, dst, tdst) in ((bv, c, tv[jj], ttv[jj]), (bg, c + NFC, tg[jj], ttg[jj])):
                        fw.op("act", lambda: nc.scalar.activation(out=dst[:, :wd], in_=banks[bk][:, 2:2 + wd], func=AF.Identity,
                                                                  scale=V[:, V_CW + 2 * 44 + cc:V_CW + 2 * 44 + cc + 1],
                                                                  bias=V[:, V_CB + cc:V_CB + cc + 1]),
                              reads=[tB[bk], tV], writes=[tdst])
                        for kk in (1, 0):
                            fw.op("dve", lambda: nc.vector.scalar_tensor_tensor(out=dst[:, :wd], in0=banks[bk][:, kk:kk + wd],
                                                                                scalar=V[:, V_CW + kk * 44 + cc:V_CW + kk * 44 + cc + 1],
                                                                                in1=dst[:, :wd], op0=ALU.mult, op1=ALU.add),
                                  reads=[tB[bk], tV], writes=[tdst])
                    fw.op("act", lambda: nc.scalar.activation(out=sg[jj][:, :wd], in_=tg[jj][:, :wd], func=AF.Silu),
                          reads=[ttg[jj]], writes=[tsg[jj]])
                    fw.op("pool", lambda: nc.gpsimd.tensor_tensor(out=gT[:, c, :wd], in0=sg[jj][:, :wd], in1=tv[jj][:, :wd], op=ALU.mult),
                          reads=[tsg[jj], ttv[jj]], writes=[tgT[c]])
                for m in range(NCH):
                    bk = 4 + (m % 2)
                    for c in range(NFC):
                        fw.op("pe", lambda: nc.tensor.matmul(banks[bk][:, :wd], wdn[:, c, m * 128:(m + 1) * 128], gT[:, c, :wd],
                                                             start=(c == 0), stop=(c == NFC - 1)),
                              reads=[tWdn, tgT[c]], **({"writes": [tB[bk]]} if c == 0 else {"acc": [tB[bk]]}))
                    fw.op("dve", lambda: nc.vector.scalar_tensor_tensor(out=x1f[j][:, m, 2:2 + wd], in0=x1f[j][:, m, 2:2 + wd], scalar=ALPHA,
                                                                        in1=banks[bk][:, :wd], op0=ALU.mult, op1=ALU.add),
                          reads=[tB[bk]], writes=[tx1f[j][m]])
                ln_block(x1f[j], wd, V_LN2G, V_LN2B, tx1f[j], (6, 7), tmp, dst_f=x1f[j], tdst_f=tx1f[j], col0=2)
                fw.dma("sp", x2T_v[:, :, c0:c0 + wd], x1f[j][:, :, 0:wd], reads=tx1f[j], semtok=tx1fload[j])
        fw.finish("sp")
        pass
    return nc


D = 1024
NCH = 8
NEG = -30000.0
C_TRI = 0
C_ONE = 128
C_ID = 256
C_SEL = 384
C_NM = 448
NCON = C_NM + 4 * 512
W_MQ, W_MK, W_MV, W_MO, W_FV, W_FQ, W_FK, W_G = 0, 128, 256, 384, 512, 640, 768, 896
NW = 900
LN_EPS = 1e-5


def build_E(S):
    TT = 512
    ntile = S // TT
    nblk = S // 128
    nc = bass.Bass("TRN2", target_bir_lowering=False)
    xT = nc.dram_tensor("xT", [D, S], F32, kind="ExternalInput").ap()
    w_sub = nc.dram_tensor("w_sub", [D, NW], F32, kind="ExternalInput").ap()
    bias_bc = nc.dram_tensor("bias_bc", [128, 516], F32, kind="ExternalInput").ap()
    bcol = nc.dram_tensor("bcol", [128, 4], F32, kind="ExternalInput").ap()
    gml = nc.dram_tensor("gml", [128, 128], F32, kind="ExternalInput").ap()
    consts = nc.dram_tensor("consts", [128, NCON], F32, kind="ExternalInput").ap()
    hml = nc.dram_tensor("hml", [S, 128], F32, kind="ExternalOutput").ap()
    hfx = nc.dram_tensor("hfx", [128, S], F32, kind="ExternalOutput").ap()
    qscr = nc.dram_tensor("qscr", [128, S], BF16, kind="Internal").ap()
    kscr = nc.dram_tensor("kscr", [128, S], BF16, kind="Internal").ap()
    fscr = nc.dram_tensor("fscr", [2, 2, S], BF16, kind="Internal").ap()
    v1scr = nc.dram_tensor("v1scr", [2, 128, nblk, 65], BF16, kind="Internal").ap()
    xT_v = xT.rearrange("(k p) n -> p k n", p=128)
    hml_v = hml.rearrange("(c p) d -> p c d", p=128)

    with contextlib.ExitStack() as st:
        fw = FW(nc, st)
        sb = lambda name, shape, dt: st.enter_context(nc.sbuf_tensor(name, shape, dt))
        banks = [st.enter_context(nc.psum_tensor(f"bank{i}", [128, 512], F32)) for i in range(8)]
        tB = fw.toks(8, "bank")
        CF = sb("CF", [128, NCON], F32)
        tC = fw.tok("C")
        fw.dma("sp", CF[:], consts, writes=[tC])
        tri = CF[:, C_TRI:C_TRI + 128]
        onesF = CF[:, C_ONE:C_ONE + 128]
        identF = CF[:, C_ID:C_ID + 128]
        negF_all = sb("negF_all", [128, nblk, 2], F32)
        tNF = fw.toks(nblk, "negF")

        with contextlib.ExitStack() as pa:
            s1 = lambda name, shape, dt: pa.enter_context(nc.sbuf_tensor(name, shape, dt))
            wsb = s1("wsb", [128, NCH, NW], BF16)
            tW = fw.tok("w")
            fw.dma("pool", wsb[:], w_sub.rearrange("(k p) n -> p k n", p=128), writes=[tW])
            bbc = s1("bbc", [128, 516], F32)
            bco = s1("bco", [128, 4], F32)
            gmls = s1("gmls", [128, 128], F32)
            tb = fw.tok("bias")
            fw.dma("sp", bbc[:], bias_bc, writes=[tb], semtok=tb)
            fw.dma("sp", bco[:], bcol, writes=[tb], semtok=tb)
            fw.dma("sp", gmls[:], gml, writes=[tb], semtok=tb)
            xb = [s1(f"xb{j}", [128, NCH, TT], BF16) for j in range(2)]
            txb = fw.toks(2, "xb")
            qT = [s1(f"qT{j}", [128, TT], BF16) for j in range(2)]
            kT = [s1(f"kT{j}", [128, TT], BF16) for j in range(2)]
            fqT = [s1(f"fqT{j}", [128, TT], BF16) for j in range(2)]
            fkT = [s1(f"fkT{j}", [128, TT], BF16) for j in range(2)]
            tqT, tkT, tfqT, tfkT = fw.toks(2), fw.toks(2), fw.toks(2), fw.toks(2)
            tm = [s1(f"tm{j}", [128, 512], F32) for j in range(2)]
            ttm = fw.toks(2)
            g4 = [s1(f"g4{j}", [128, 4], F32) for j in range(2)]
            tg4 = fw.toks(2)
            lsp = [s1(f"lsp{j}", [128, 4], F32) for j in range(2)]
            tlsp = fw.toks(2)
            for j in range(2):
                fw.op("dve", lambda: nc.vector.memset(lsp[j][:], 0.0), writes=[tlsp[j]])
            esc = [s1(f"esc{j}", [128, 8], F32) for j in range(2)]
            tesc = [fw.toks(8) for j in range(2)]
            kw = [s1(f"kw{j}", [128, 128], BF16) for j in range(2)]
            v1 = [s1(f"v1{j}", [128, 129], BF16) for j in range(2)]
            tkw, tv1 = fw.toks(2), fw.toks(2)
            osig = [s1(f"osig{j}", [128, 128], F32) for j in range(2)]
            tosig = fw.toks(2)
            pT = [s1(f"pT{j}", [128, 128], BF16) for j in range(2)]
            tpT = fw.toks(2)
            CN = s1("CN", [128, 129], F32)
            CNb = s1("CNb", [128, 129], BF16)
            tCN, tCNb = fw.tok(), fw.tok()
            fw.op("dve", lambda: nc.vector.memset(CN[:], 0.0), writes=[tCN])
            fw.op("dve", lambda: nc.vector.memset(CNb[:], 0.0), writes=[tCNb])
            for j in range(2):
                fw.op("dve", lambda: nc.vector.memset(v1[j][:, 128:129], 1.0), writes=[tv1[j]])
            v1s = [s1(f"v1s{j}", [128, 2, 4, 65], BF16) for j in range(2)]
            tv1s = fw.toks(2)
            for j in range(2):
                fw.op("dve", lambda: nc.vector.memset(v1s[j][:], 1.0), writes=[tv1s[j]])
            carry = [s1(f"carry{j}", [128, 2], F32) for j in range(2)]
            tcar = fw.toks(2)
            fw.op("dve", lambda: nc.vector.memset(carry[0][:], 0.0), writes=[tcar[0]])
            hh = [s1(f"hh{j}", [128, 128], F32) for j in range(2)]
            thh = fw.toks(2)
            st6 = [s1(f"st6{j}", [128, 6], F32) for j in range(2)]
            mv2 = [s1(f"mv2{j}", [128, 2], F32) for j in range(2)]
            tst6, tmv2 = fw.toks(2), fw.toks(2)
            epsc = s1("epsc", [128, 2], F32)
            tepsc = fw.tok()
            fw.op("dve", lambda: nc.vector.memset(epsc[:, 0:1], LN_EPS), writes=[tepsc])
            fw.op("dve", lambda: nc.vector.memset(epsc[:, 1:2], 1.0), writes=[tepsc])
            hout = [s1(f"hout{j}", [128, 4, 128], F32) for j in range(2)]
            thout = fw.toks(2)
            frow = [s1(f"frow{j}", [2, 2, TT], BF16) for j in range(2)]
            tfrow = fw.toks(2)
            tqs, tks, tfs, tv1scr = fw.tok(), fw.tok(), fw.tok(), fw.tok()

            for ti in range(ntile):
                j = ti % 2
                c0 = ti * TT
                fw.dma("pool", xb[j][:], xT_v[:, :, c0:c0 + TT], writes=[txb[j]])
                for gi, (wc, dst, tdst, scale) in enumerate(((W_MQ, qT[j], tqT[j], 128 ** -0.5), (W_MK, kT[j], tkT[j], 1.0),
                                                              (W_FQ, fqT[j], tfqT[j], 0.125), (W_FK, fkT[j], tfkT[j], 1.0))):
                    bk = gi % 2
                    for k in range(NCH):
                        fw.op("pe", lambda: nc.tensor.matmul(banks[bk][:, :], wsb[:, k, wc:wc + 128], xb[j][:, k, :], start=(k == 0), stop=(k == NCH - 1)),
                              reads=[tW, txb[j]], **({"writes": [tB[bk]]} if k == 0 else {"acc": [tB[bk]]}))
                    fw.op("dve", lambda: nc.vector.tensor_scalar(out=dst[:], in0=banks[bk][:, :], scalar1=bco[:, gi:gi + 1], scalar2=scale,
                                                                 op0=ALU.add, op1=ALU.mult),
                          reads=[tB[bk], tb], writes=[tdst])
                fw.dma("sp", qscr[:, c0:c0 + TT], fqT[j][:], reads=[tfqT[j]], semtok=tqs)
                fw.dma("sp", kscr[:, c0:c0 + TT], fkT[j][:], reads=[tfkT[j]], semtok=tks)
                for cc in range(4):
                    blk = ti * 4 + cc
                    jc = blk % 2
                    cs = slice(cc * 128, (cc + 1) * 128)
                    bkt = 2 + jc
                    for k in range(NCH):
                        fw.op("pe", lambda: nc.tensor.matmul(banks[bkt][:, :], xb[j][:, k, cs], wsb[:, k, W_MK:W_MK + 512], start=(k == 0), stop=(k == NCH - 1)),
                              reads=[tW, txb[j]], **({"writes": [tB[bkt]]} if k == 0 else {"acc": [tB[bkt]]}))
                    for k in range(NCH):
                        fw.op("pe", lambda: nc.tensor.matmul(banks[4][:, 0:4], xb[j][:, k, cs], wsb[:, k, W_G:W_G + 4], start=(k == 0), stop=(k == NCH - 1)),
                              reads=[tW, txb[j]], **({"writes": [tB[4]]} if k == 0 else {"acc": [tB[4]]}))
                    fw.op("dve", lambda: nc.vector.tensor_tensor(out=tm[jc][:], in0=banks[bkt][:, :], in1=bbc[:, 0:512], op=ALU.add),
                          reads=[tB[bkt], tb], writes=[ttm[jc]])
                    fw.op("dve", lambda: nc.vector.tensor_tensor(out=g4[jc][:], in0=banks[4][:, 0:4], in1=bbc[:, 512:516], op=ALU.add),
                          reads=[tB[4], tb], writes=[tg4[jc]])
                    fw.op("act", lambda: nc.scalar.activation(out=lsp[jc][:, 0:3], in_=g4[jc][:, 1:4], func=AF.Exp, scale=-1.0),
                          reads=[tg4[jc]], writes=[tlsp[jc]])
                    fw.op("act", lambda: nc.scalar.activation(out=lsp[jc][:, 0:3], in_=lsp[jc][:, 0:3], func=AF.Ln, bias=epsc[:, 1:2]),
                          reads=[tepsc], writes=[tlsp[jc]])
                    fw.op("pe", lambda: nc.tensor.matmul(banks[5][:, 0:4], tri, lsp[jc][:], start=True, stop=True),
                          reads=[tC, tlsp[jc]], writes=[tB[5]])
                    fw.op("pe", lambda: nc.tensor.matmul(banks[5][:, 8:12], onesF, lsp[jc][:], start=True, stop=True),
                          reads=[tC, tlsp[jc]], acc=[tB[5]])
                    E = esc[jc]
                    tE = tesc[jc]
                    fw.op("act", lambda: nc.scalar.activation(out=E[:, 0:1], in_=banks[5][:, 0:1], func=AF.Exp, bias=g4[jc][:, 0:1]),
                          reads=[tB[5], tg4[jc]], writes=[tE[0]])
                    fw.op("act", lambda: nc.scalar.activation(out=E[:, 1:2], in_=banks[5][:, 0:1], func=AF.Exp, scale=-1.0),
                          reads=[tB[5]], writes=[tE[1]])
                    fw.op("dve", lambda: nc.vector.tensor_tensor(out=E[:, 4:5], in0=g4[jc][:, 0:1], in1=banks[5][:, 8:9], op=ALU.subtract),
                          reads=[tB[5], tg4[jc]], writes=[tE[4]])
                    fw.op("act", lambda: nc.scalar.activation(out=E[:, 2:3], in_=banks[5][:, 0:1], func=AF.Exp, bias=E[:, 4:5]),
                          reads=[tB[5], tE[4]], writes=[tE[2]])
                    fw.op("act", lambda: nc.scalar.activation(out=E[:, 3:4], in_=banks[5][:, 8:9], func=AF.Exp, scale=-1.0),
                          reads=[tB[5]], writes=[tE[3]])
                    cj, cn = blk % 2, (blk + 1) % 2
                    fw.op("dve", lambda: nc.vector.tensor_tensor(out=negF_all[:, blk, :], in0=banks[5][:, 1:3], in1=carry[cj][:], op=ALU.subtract),
                          reads=[tB[5], tcar[cj]], writes=[tNF[blk]])
                    fw.op("dve", lambda: nc.vector.tensor_tensor(out=carry[cn][:], in0=carry[cj][:], in1=banks[5][:, 9:11], op=ALU.subtract),
                          reads=[tB[5], tcar[cj]], writes=[tcar[cn]])
                    fw.op("dve", lambda: nc.vector.tensor_scalar(out=kw[jc][:], in0=tm[jc][:, 0:128], scalar1=E[:, 2:3], scalar2=None, op0=ALU.mult),
                          reads=[ttm[jc], tE[2]], writes=[tkw[jc]])
                    fw.op("pool", lambda: nc.gpsimd.tensor_copy(out=v1[jc][:, 0:128], in_=tm[jc][:, 128:256]),
                          reads=[ttm[jc]], writes=[tv1[jc]])
                    fw.op("act", lambda: nc.scalar.activation(out=osig[jc][:], in_=tm[jc][:, 256:384], func=AF.Sigmoid),
                          reads=[ttm[jc]], writes=[tosig[jc]])
                    fw.op("pool", lambda: nc.gpsimd.tensor_copy(out=v1s[j][:, :, cc, 0:64], in_=tm[jc][:, 384:512].rearrange("p (h d) -> p h d", h=2)),
                          reads=[ttm[jc]], writes=[tv1s[j]])
                    fw.op("pe", lambda: nc.tensor.matmul(banks[6][:, 0:128], kT[j][:, cs], qT[j][:, cs], start=True, stop=True),
                          reads=[tkT[j], tqT[j]], writes=[tB[6]])
                    fw.op("dve", lambda: nc.vector.scalar_tensor_tensor(out=pT[jc][:], in0=banks[6][:, 0:128], scalar=E[:, 0:1], in1=tri,
                                                                        op0=ALU.mult, op1=ALU.mult),
                          reads=[tB[6], tE[0], tC], writes=[tpT[jc]])
                    fw.op("pe", lambda: nc.tensor.matmul(banks[7][:, 0:129], pT[jc][:], v1[jc][:], start=True, stop=False),
                          reads=[tpT[jc], tv1[jc]], writes=[tB[7]])
                    fw.op("pe", lambda: nc.tensor.matmul(banks[7][:, 0:129], qT[j][:, cs], CNb[:], start=False, stop=True),
                          reads=[tqT[j], tCNb], acc=[tB[7]])
                    fw.op("pe", lambda: nc.tensor.matmul(banks[6][:, 256:385], kw[jc][:], v1[jc][:], start=True, stop=True),
                          reads=[tkw[jc], tv1[jc]], writes=[tB[6]])
                    fw.op("dve", lambda: nc.vector.scalar_tensor_tensor(out=CN[:], in0=CN[:], scalar=E[:, 3:4], in1=banks[6][:, 256:385],
                                                                        op0=ALU.mult, op1=ALU.add),
                          reads=[tB[6], tE[3]], writes=[tCN])
                    fw.op("act", lambda: nc.scalar.activation(out=CNb[:], in_=CN[:], func=AF.Copy), reads=[tCN], writes=[tCNb])
                    fw.op("dve", lambda: nc.vector.tensor_tensor(out=E[:, 5:6], in0=banks[7][:, 128:129], in1=E[:, 1:2], op=ALU.mult),
                          reads=[tB[7], tE[1]], writes=[tE[5]])
                    fw.op("dve", lambda: nc.vector.tensor_scalar(out=E[:, 6:7], in0=E[:, 5:6], scalar1=1.0, scalar2=None, op0=ALU.max),
                          reads=[tE[5]], writes=[tE[6]])
                    fw.op("dve", lambda: nc.vector.scalar_tensor_tensor(out=E[:, 5:6], in0=E[:, 5:6], scalar=-1.0, in1=E[:, 6:7], op0=ALU.mult, op1=ALU.max),
                          reads=[tE[6]], writes=[tE[5]])
                    fw.op("dve", lambda: nc.vector.reciprocal(out=E[:, 5:6], in_=E[:, 5:6]), writes=[tE[5]])
                    fw.op("dve", lambda: nc.vector.tensor_tensor(out=E[:, 6:7], in0=E[:, 5:6], in1=E[:, 1:2], op=ALU.mult),
                          reads=[tE[5], tE[1]], writes=[tE[6]])
                    fw.op("act", lambda: nc.scalar.activation(out=hh[jc][:], in_=banks[7][:, 0:128], func=AF.Copy, scale=E[:, 6:7]),
                          reads=[tB[7], tE[6]], writes=[thh[jc]])
                    fw.op("dve", lambda: nc.vector.bn_stats(out=st6[jc][:], in_=hh[jc][:]), reads=[thh[jc]], writes=[tst6[jc]])
                    fw.op("dve", lambda: nc.vector.bn_aggr(out=mv2[jc][:], in_=st6[jc][:]), reads=[tst6[jc]], writes=[tmv2[jc]])
                    fw.op("act", lambda: nc.scalar.activation(out=E[:, 7:8], in_=mv2[jc][:, 1:2], func=AF.Sqrt, bias=epsc[:, 0:1]),
                          reads=[tmv2[jc], tepsc], writes=[tE[7]])
                    fw.op("dve", lambda: nc.vector.reciprocal(out=E[:, 7:8], in_=E[:, 7:8]), writes=[tE[7]])
                    fw.op("dve", lambda: nc.vector.tensor_scalar(out=hh[jc][:], in0=hh[jc][:], scalar1=mv2[jc][:, 0:1], scalar2=E[:, 7:8],
                                                                 op0=ALU.subtract, op1=ALU.mult),
                          reads=[tmv2[jc], tE[7]], writes=[thh[jc]])
                    fw.op("pool", lambda: nc.gpsimd.tensor_tensor(out=hh[jc][:], in0=hh[jc][:], in1=gmls[:], op=ALU.mult),
                          reads=[tb], writes=[thh[jc]])
                    fw.op("dve", lambda: nc.vector.tensor_tensor(out=hout[j][:, cc, :], in0=hh[jc][:], in1=osig[jc][:], op=ALU.mult),
                          reads=[thh[jc], tosig[jc]], writes=[thout[j]])
                fw.dma("sp", hml_v[:, ti * 4:(ti + 1) * 4, :], hout[j][:], reads=[thout[j]])
                for h in range(2):
                    fw.dma("sp", v1scr[h, :, ti * 4:(ti + 1) * 4, :], v1s[j][:, h, :, :], reads=[tv1s[j]], semtok=tv1scr)
                for cc in range(4):
                    blk = ti * 4 + cc
                    fw.op("pe", lambda: nc.tensor.transpose(banks[0][0:2, cc * 128:(cc + 1) * 128], negF_all[:, blk, :], identF),
                          reads=[tNF[blk], tC], **({"writes": [tB[0]]} if cc == 0 else {"acc": [tB[0]]}))
                fw.op("act", lambda: nc.scalar.activation(out=frow[j][:, 0, :], in_=banks[0][0:2, :], func=AF.Copy, scale=-1.0),
                      reads=[tB[0]], writes=[tfrow[j]])
                fw.op("dve", lambda: nc.vector.scalar_tensor_tensor(out=frow[j][:, 1, :], in0=banks[0][0:2, :], scalar=-1.0, in1=frow[j][:, 0, :],
                                                                    op0=ALU.mult, op1=ALU.subtract),
                      reads=[tB[0]], writes=[tfrow[j]])
                for hl in range(2):
                    fw.dma("sp", fscr[hl, :, c0:c0 + TT], frow[j][:, hl, :], reads=[tfrow[j]], semtok=tfs)
            scr_deps = [(t.dsem, t.dcnt) for t in (tqs, tks, tfs, tv1scr)]
        fw.barrier()
        with contextlib.ExitStack() as pb:
            s2 = lambda name, shape, dt: pb.enter_context(nc.sbuf_tensor(name, shape, dt))
            qa = s2("qa", [66, S], BF16)
            ka = s2("ka", [66, S], BF16)
            V1 = s2("V1", [128, nblk, 65], BF16)
            tqa, tka, tV1 = fw.tok(), fw.tok(), fw.tok()
            nmb = s2("nmb", [128, 4, 512], BF16)
            idb = s2("idb", [128, 128], BF16)
            tnm = fw.tok()
            fw.op("dve", lambda: nc.vector.tensor_copy(out=nmb[:], in_=CF[:, C_NM:C_NM + 2048].rearrange("p (j t) -> p j t", j=4)), reads=[tC], writes=[tnm])
            fw.op("dve", lambda: nc.vector.tensor_copy(out=idb[:], in_=identF), reads=[tC], writes=[tnm])
            PT = [s2(f"PT{j}", [128, 512], BF16) for j in range(3)]
            tPT = fw.toks(3)
            osb = [s2(f"osb{j}", [65, 512], F32) for j in range(2)]
            tosb = fw.toks(2)
            rden = [s2(f"rden{j}", [64, 512], F32) for j in range(2)]
            trden = fw.toks(2)
            hfo = [s2(f"hfo{j}", [64, 512], F32) for j in range(2)]
            thfo = fw.toks(2)
            nq = S // 512
            it = 0
            for h in range(2):
                fw.dma("sp", qa[0:64, :], qscr[h * 64:(h + 1) * 64, :], writes=[tqa], semtok=tqa)
                fw.dma("sp", qa[64:65, :], fscr[0, h:h + 1, :], writes=[tqa], semtok=tqa)
                fw.dma("sp", qa[65:66, :], fscr[1, h:h + 1, :], writes=[tqa], semtok=tqa)
                fw.dma("sp", ka[0:64, :], kscr[h * 64:(h + 1) * 64, :], writes=[tka], semtok=tka)
                fw.op("dve", lambda: nc.vector.memset(ka[64:66, :], 1.0), writes=[tka])
                fw.dma("sp", V1[:], v1scr[h], writes=[tV1])
                for qi in range(nq):
                    jo = qi % 2
                    bo = 3 + jo
                    nkb = 4 * qi + 4
                    qs = slice(qi * 512, (qi + 1) * 512)
                    for kb in range(nkb):
                        js = it % 3
                        it += 1
                        ks = slice(kb * 128, (kb + 1) * 128)
                        diag = kb >= 4 * qi
                        fw.op("pe", lambda: nc.tensor.matmul(banks[js][:, :], ka[:, ks], qa[:, qs], start=True, stop=not diag),
                              reads=[tka, tqa], writes=[tB[js]])
                        if diag:
                            fw.op("pe", lambda: nc.tensor.matmul(banks[js][:, :], idb[:], nmb[:, kb - 4 * qi, :], start=False, stop=True),
                                  reads=[tnm], acc=[tB[js]])
                        fw.op("act", lambda: nc.scalar.activation(out=PT[js][:], in_=banks[js][:, :], func=AF.Exp, bias=negF_all[:, kb, h:h + 1]),
                              reads=[tB[js], tNF[kb]], writes=[tPT[js]])
                        fw.op("pe", lambda: nc.tensor.matmul(banks[bo][0:65, :], V1[:, kb, :], PT[js][:], start=(kb == 0), stop=(kb == nkb - 1)),
                              reads=[tV1, tPT[js]], **({"writes": [tB[bo]]} if kb == 0 else {"acc": [tB[bo]]}))
                    fw.op("act", lambda: nc.scalar.activation(out=osb[jo][:], in_=banks[bo][0:65, :], func=AF.Copy), reads=[tB[bo]], writes=[tosb[jo]])
                    fw.op("pe", lambda: nc.tensor.matmul(banks[5][0:64, :], CF[0:65, C_SEL:C_SEL + 64], osb[jo][:], start=True, stop=True),
                          reads=[tC, tosb[jo]], writes=[tB[5]])
                    fw.op("dve", lambda: nc.vector.reciprocal(out=rden[jo][:], in_=banks[5][0:64, :]), reads=[tB[5]], writes=[trden[jo]])
                    fw.op("dve", lambda: nc.vector.tensor_tensor(out=hfo[jo][:], in0=osb[jo][0:64, :], in1=rden[jo][:], op=ALU.mult),
                          reads=[tosb[jo], trden[jo]], writes=[thfo[jo]])
                    fw.dma("sp", hfx[h * 64:(h + 1) * 64, qs], hfo[jo][:], reads=[thfo[jo]])
        fw.finish("sp")
        pass
    return nc


D = 1024
NCH = 8
NWO = 642
WO_X, WO_B, WO_C, WO_P, WO_Z, WO_DT = 0, 128, 256, 384, 512, 640
VO_CW, VO_CB, VO_DTB, VO_ALOG, VO_DSK, VO_PB, VO_PS, VO_SELW, VO_CORR = 0, 12, 15, 17, 19, 21, 22, 23, 27
NVO = 43
BIG = 1e30


def build_O(S):
    TT = 512
    ntile = S // TT
    nc = bass.Bass("TRN2", target_bir_lowering=False)
    xT = nc.dram_tensor("xT", [D, S], F32, kind="ExternalInput").ap()
    w_sub = nc.dram_tensor("w_sub", [D, NWO], F32, kind="ExternalInput").ap()
    vecs = nc.dram_tensor("vecs", [128, NVO], F32, kind="ExternalInput").ap()
    wgrp = nc.dram_tensor("wgrp", [128, 128], F32, kind="ExternalInput").ap()
    consts = nc.dram_tensor("consts", [128, NCON], F32, kind="ExternalInput").ap()
    hss = nc.dram_tensor("hss", [S, 128], F32, kind="ExternalOutput").ap()
    hpl = nc.dram_tensor("hpl", [128, S], F32, kind="ExternalOutput").ap()
    xT_v = xT.rearrange("(k p) n -> p k n", p=128)
    hss_v = hss.rearrange("(c p) d -> p c d", p=128)

    with contextlib.ExitStack() as st:
        fw = FW(nc, st)
        sb = lambda name, shape, dt: st.enter_context(nc.sbuf_tensor(name, shape, dt))
        banks = [st.enter_context(nc.psum_tensor(f"bank{i}", [128, 512], F32)) for i in range(8)]
        tB = fw.toks(8, "bank")
        CF = sb("CF", [128, NCON], F32)
        tC = fw.tok("C")
        fw.dma("sp", CF[:], consts, writes=[tC])
        tri = CF[:, C_TRI:C_TRI + 128]
        onesF = CF[:, C_ONE:C_ONE + 128]
        identF = CF[:, C_ID:C_ID + 128]
        V = sb("V", [128, NVO], F32)
        tV = fw.tok("V")
        fw.dma("sp", V[:], vecs, writes=[tV])
        wsb = sb("wsb", [128, NCH, NWO], BF16)
        tW = fw.tok("w")
        fw.dma("pool", wsb[:], w_sub.rearrange("(k p) n -> p k n", p=128), writes=[tW])
        wg = sb("wg", [128, 128], BF16)
        tWg = fw.tok("wg")
        fw.dma("pool", wg[:], wgrp, writes=[tWg])
        negA = sb("negA", [128, 2], F32)
        tnA = fw.tok()
        fw.op("act", lambda: nc.scalar.activation(out=negA[:], in_=V[:, VO_ALOG:VO_ALOG + 2], func=AF.Exp), reads=[tV], writes=[tnA])
        posm = sb("posm", [128, 128], F32)
        tposm = fw.tok()
        fw.op("dve", lambda: nc.vector.tensor_scalar(out=posm[:], in0=tri, scalar1=-30000.0, scalar2=30000.0, op0=ALU.mult, op1=ALU.add), reads=[tC], writes=[tposm])
        yo = [sb(f"yo{j}", [128, 128], F32) for j in range(2)]
        tyo = fw.toks(2)
        one1 = sb("one1", [128, 1], F32)
        tone = fw.tok()
        fw.op("dve", lambda: nc.vector.memset(one1[:], 1.0), writes=[tone])

        xb = [sb(f"xb{j}", [128, NCH, TT], BF16) for j in range(2)]
        txb = fw.toks(2)
        ubuf = [sb(f"ubuf{g}", [128, 3 + TT], F32) for g in range(3)]
        tub = fw.toks(3)
        for g in range(3):
            fw.op("dve", lambda: nc.vector.memset(ubuf[g][:, 0:3], 0.0), writes=[tub[g]])
        cacc = [sb(f"cacc{g}", [128, TT], F32) for g in range(3)]
        tcacc = fw.toks(3)
        xsT = sb("xsT", [128, TT], F32)
        BTf = sb("BTf", [128, TT], F32)
        BTb = sb("BTb", [128, TT], BF16)
        CTb = sb("CTb", [128, TT], BF16)
        txsT, tBTf, tBTb, tCTb = fw.tok(), fw.tok(), fw.tok(), fw.tok()
        pbuf = sb("pbuf", [128, 15 + TT], F32)
        tpb = fw.tok()
        fw.op("dve", lambda: nc.vector.memset(pbuf[:, 0:15], 0.0), writes=[tpb])
        sl = [sb(f"sl{i}", [128, 15 + TT], F32) for i in range(4)]
        tsl = fw.toks(4)
        pacc = sb("pacc", [128, TT], F32)
        tpacc = fw.tok()
        pooled = sb("pooled", [128, TT], BF16)
        tpooled = fw.tok()
        pout = [sb(f"pout{j}", [128, TT], F32) for j in range(2)]
        tpout = fw.toks(2)
        zd = [sb(f"zd{j}", [128, 130], F32) for j in range(2)]
        tzd = fw.toks(2)
        zs = [sb(f"zs{j}", [128, 128], F32) for j in range(2)]
        tzs = fw.toks(2)
        sc = [sb(f"sc{j}", [128, 16], F32) for j in range(2)]
        tsc = [fw.toks(8) for j in range(2)]
        labc = [sb(f"labc{j}", [128, 2, 128], F32) for j in range(2)]
        tlabc = fw.toks(2)
        xtm = [sb(f"xtm{j}", [128, 128], F32) for j in range(2)]
        Btm = [sb(f"Btm{j}", [128, 128], BF16) for j in range(2)]
        txtm, tBtm = fw.toks(2), fw.toks(2)
        xdt = [sb(f"xdt{j}", [128, 128], BF16) for j in range(2)]
        xdtd = [sb(f"xdtd{j}", [128, 128], BF16) for j in range(2)]
        txdt, txdtd = fw.toks(2), fw.toks(2)
        Gm = [sb(f"Gm{j}", [128, 128], F32) for j in range(2)]
        tGm = fw.toks(2)
        Lr = [sb(f"Lr{j}", [128, 2, 128], F32) for j in range(2)]
        tLr = fw.toks(2)
        WT = [sb(f"WT{j}", [128, 2, 128], BF16) for j in range(2)]
        tWT = fw.toks(2)
        ECs = [sb(f"ECs{j}", [128, 2, 128], F32) for j in range(2)]
        tECs = fw.toks(2)
        CTs = [sb(f"CTs{j}", [128, 2, 128], BF16) for j in range(2)]
        tCTs = fw.toks(2)
        H = sb("H", [128, 128], F32)
        Hb = sb("Hb", [128, 128], BF16)
        tH, tHb = fw.tok(), fw.tok()
        fw.op("dve", lambda: nc.vector.memset(H[:], 0.0), writes=[tH])
        fw.op("dve", lambda: nc.vector.memset(Hb[:], 0.0), writes=[tHb])
        yy = [sb(f"yy{j}", [128, 128], F32) for j in range(2)]
        tyy = fw.toks(2)
        hout = [sb(f"hout{j}", [128, 4, 128], F32) for j in range(2)]
        thout = fw.toks(2)

        for ti in range(ntile):
            j = ti % 2
            c0 = ti * TT
            fw.dma("pool", xb[j][:], xT_v[:, :, c0:c0 + TT], writes=[txb[j]])
            for g, wc in enumerate((WO_X, WO_B, WO_C)):
                bk = g % 2
                for k in range(NCH):
                    fw.op("pe", lambda: nc.tensor.matmul(banks[bk][:, :], wsb[:, k, wc:wc + 128], xb[j][:, k, :], start=(k == 0), stop=(k == NCH - 1)),
                          reads=[tW, txb[j]], **({"writes": [tB[bk]]} if k == 0 else {"acc": [tB[bk]]}))
                fw.op("act", lambda: nc.scalar.activation(out=ubuf[g][:, 3:3 + TT], in_=banks[bk][:, :], func=AF.Copy), reads=[tB[bk]], writes=[tub[g]])
                cw = lambda k: V[:, VO_CW + g * 4 + k:VO_CW + g * 4 + k + 1]
                fw.op("dve", lambda: nc.vector.tensor_scalar(out=cacc[g][:], in0=ubuf[g][:, 3:3 + TT], scalar1=cw(3), scalar2=V[:, VO_CB + g:VO_CB + g + 1],
                                                             op0=ALU.mult, op1=ALU.add),
                      reads=[tub[g], tV], writes=[tcacc[g]])
                for k in (2, 1, 0):
                    fw.op("dve", lambda: nc.vector.scalar_tensor_tensor(out=cacc[g][:], in0=ubuf[g][:, k:k + TT], scalar=cw(k), in1=cacc[g][:],
                                                                        op0=ALU.mult, op1=ALU.add),
                          reads=[tub[g], tV], writes=[tcacc[g]])
                fw.op("pool", lambda: nc.gpsimd.tensor_copy(out=ubuf[g][:, 0:3], in_=ubuf[g][:, TT:TT + 3]), reads=[], writes=[tub[g]])
                if g == 0:
                    fw.op("act", lambda: nc.scalar.activation(out=xsT[:], in_=cacc[g][:], func=AF.Silu), reads=[tcacc[g]], writes=[txsT])
                elif g == 1:
                    fw.op("act", lambda: nc.scalar.activation(out=BTf[:], in_=cacc[g][:], func=AF.Silu), reads=[tcacc[g]], writes=[tBTf])
                    fw.op("pool", lambda: nc.gpsimd.tensor_copy(out=BTb[:], in_=BTf[:]), reads=[tBTf], writes=[tBTb])
                else:
                    fw.op("act", lambda: nc.scalar.activation(out=CTb[:], in_=cacc[g][:], func=AF.Silu), reads=[tcacc[g]], writes=[tCTb])
            bk = 1
            for k in range(NCH):
                fw.op("pe", lambda: nc.tensor.matmul(banks[bk][:, :], wsb[:, k, WO_P:WO_P + 128], xb[j][:, k, :], start=(k == 0), stop=(k == NCH - 1)),
                      reads=[tW, txb[j]], **({"writes": [tB[bk]]} if k == 0 else {"acc": [tB[bk]]}))
            fw.op("act", lambda: nc.scalar.activation(out=pbuf[:, 15:15 + TT], in_=banks[bk][:, :], func=AF.Copy), reads=[tB[bk]], writes=[tpb])
            prev, tprev = pbuf, tpb
            for i, sh in enumerate((1, 2, 4, 8)):
                lo = 2 * sh - 1
                fw.op("pool", lambda: nc.gpsimd.tensor_tensor(out=sl[i][:, lo:15 + TT], in0=prev[:, lo:15 + TT], in1=prev[:, lo - sh:15 + TT - sh], op=ALU.add),
                      reads=[tprev], writes=[tsl[i]])
                prev, tprev = sl[i], tsl[i]
            fw.op("dve", lambda: nc.vector.tensor_scalar(out=pacc[:], in0=sl[0][:, 15:15 + TT], scalar1=V[:, VO_SELW:VO_SELW + 1], scalar2=None, op0=ALU.mult),
                  reads=[tsl[0], tV], writes=[tpacc])
            for i in (1, 2, 3):
                fw.op("dve", lambda: nc.vector.scalar_tensor_tensor(out=pacc[:], in0=sl[i][:, 15:15 + TT], scalar=V[:, VO_SELW + i:VO_SELW + i + 1], in1=pacc[:],
                                                                    op0=ALU.mult, op1=ALU.add),
                      reads=[tsl[i], tV], writes=[tpacc])
            if ti == 0:
                fw.op("dve", lambda: nc.vector.tensor_tensor(out=pacc[:, 0:16], in0=pacc[:, 0:16], in1=V[:, VO_CORR:VO_CORR + 16], op=ALU.mult),
                      reads=[tV], writes=[tpacc])
            fw.op("dve", lambda: nc.vector.tensor_tensor(out=pooled[:], in0=pacc[:], in1=pbuf[:, 15:15 + TT], op=ALU.subtract),
                  reads=[tpacc, tpb], writes=[tpooled])
            fw.op("pool", lambda: nc.gpsimd.tensor_copy(out=pbuf[:, 0:15], in_=pbuf[:, TT:TT + 15]), reads=[], writes=[tpb])
            fw.op("pe", lambda: nc.tensor.matmul(banks[0][:, :], wg[:], pooled[:], start=True, stop=True), reads=[tWg, tpooled], writes=[tB[0]])
            fw.op("dve", lambda: nc.vector.tensor_scalar(out=pout[j][:], in0=banks[0][:, :], scalar1=V[:, VO_PB:VO_PB + 1], scalar2=V[:, VO_PS:VO_PS + 1],
                                                         op0=ALU.add, op1=ALU.mult),
                  reads=[tB[0], tV], writes=[tpout[j]])
            fw.dma("sp", hpl[:, c0:c0 + TT], pout[j][:], reads=[tpout[j]])
            for cc in range(4):
                blk = ti * 4 + cc
                jc = blk % 2
                cs = slice(cc * 128, (cc + 1) * 128)
                Sc, tS = sc[jc], tsc[jc]
                for k in range(NCH):
                    fw.op("pe", lambda: nc.tensor.matmul(banks[2][:, 0:130], xb[j][:, k, cs], wsb[:, k, WO_Z:WO_Z + 130], start=(k == 0), stop=(k == NCH - 1)),
                          reads=[tW, txb[j]], **({"writes": [tB[2]]} if k == 0 else {"acc": [tB[2]]}))
                fw.op("act", lambda: nc.scalar.activation(out=zd[jc][:], in_=banks[2][:, 0:130], func=AF.Copy), reads=[tB[2]], writes=[tzd[jc]])
                fw.op("act", lambda: nc.scalar.activation(out=zs[jc][:], in_=zd[jc][:, 0:128], func=AF.Silu), reads=[tzd[jc]], writes=[tzs[jc]])
                fw.op("dve", lambda: nc.vector.tensor_tensor(out=Sc[:, 0:2], in0=zd[jc][:, 128:130], in1=V[:, VO_DTB:VO_DTB + 2], op=ALU.add),
                      reads=[tzd[jc], tV], writes=[tS[0]])
                fw.op("act", lambda: nc.scalar.activation(out=Sc[:, 0:2], in_=Sc[:, 0:2], func=AF.Exp), reads=[], writes=[tS[0]])
                fw.op("act", lambda: nc.scalar.activation(out=Sc[:, 0:2], in_=Sc[:, 0:2], func=AF.Ln, bias=one1[:, 0:1]), reads=[tone], writes=[tS[0]])
                fw.op("dve", lambda: nc.vector.tensor_tensor(out=Sc[:, 2:4], in0=Sc[:, 0:2], in1=negA[:], op=ALU.mult), reads=[tS[0], tnA], writes=[tS[1]])
                fw.op("pe", lambda: nc.tensor.matmul(banks[4][:, 0:2], tri, Sc[:, 2:4], start=True, stop=True), reads=[tC, tS[1]], writes=[tB[4]])
                fw.op("pe", lambda: nc.tensor.matmul(banks[4][:, 8:10], onesF, Sc[:, 2:4], start=True, stop=True), reads=[tC, tS[1]], acc=[tB[4]])
                for h in range(2):
                    fw.op("pool", lambda: nc.gpsimd.tensor_scalar(out=labc[jc][:, h, :], in0=onesF, scalar1=Sc[:, 2 + h:3 + h], scalar2=None, op0=ALU.mult),
                          reads=[tS[1], tC], writes=[tlabc[jc]])
                for h in range(2):
                    fw.op("pe", lambda: nc.tensor.matmul(banks[5][:, h * 128:(h + 1) * 128], labc[jc][:, h, :], tri, start=True, stop=False),
                          reads=[tlabc[jc], tC], **({"writes": [tB[5]]} if h == 0 else {"acc": [tB[5]]}))
                    fw.op("pe", lambda: nc.tensor.matmul(banks[5][:, h * 128:(h + 1) * 128], identF, posm[:], start=False, stop=True),
                          reads=[tposm, tC], acc=[tB[5]])
                fw.op("pe", lambda: nc.tensor.matmul(banks[5][:, 256:384], BTb[:, cs], CTb[:, cs], start=True, stop=True),
                      reads=[tBTb, tCTb], acc=[tB[5]])
                fw.op("dve", lambda: nc.vector.tensor_copy(out=Sc[:, 12:14], in_=banks[4][:, 0:2]), reads=[tB[4]], writes=[tS[6]])
                fw.op("dve", lambda: nc.vector.tensor_scalar(out=Sc[:, 4:6], in0=banks[4][:, 8:10], scalar1=-1.0, scalar2=None, op0=ALU.mult),
                      reads=[tB[4]], writes=[tS[2]])
                for h in range(2):
                    fw.op("act", lambda: nc.scalar.activation(out=Sc[:, 6 + h:7 + h], in_=banks[4][:, h:h + 1], func=AF.Exp, bias=Sc[:, 4 + h:5 + h]),
                          reads=[tB[4], tS[2]], writes=[tS[3]])
                fw.op("act", lambda: nc.scalar.activation(out=Sc[:, 10:12], in_=banks[4][:, 8:10], func=AF.Exp, scale=-1.0), reads=[tB[4]], writes=[tS[5]])
                fw.op("dve", lambda: nc.vector.tensor_tensor(out=Sc[:, 8:10], in0=Sc[:, 0:2], in1=Sc[:, 6:8], op=ALU.mult), reads=[tS[0], tS[3]], writes=[tS[4]])
                for h in range(2):
                    fw.op("act", lambda: nc.scalar.activation(out=Lr[jc][:, h, :], in_=banks[5][:, h * 128:(h + 1) * 128], func=AF.Exp, scale=-1.0,
                                                              bias=Sc[:, 12 + h:13 + h]),
                          reads=[tB[5], tS[6]], writes=[tLr[jc]])
                fw.op("act", lambda: nc.scalar.activation(out=Sc[:, 14:16], in_=banks[4][:, 0:2], func=AF.Exp, scale=-1.0), reads=[tB[4]], writes=[tS[7]])
                for h in range(2):
                    fw.op("dve", lambda: nc.vector.tensor_tensor(out=WT[jc][:, h, :], in0=Lr[jc][:, h, :], in1=banks[5][:, 256:384], op=ALU.mult),
                          reads=[tLr[jc], tB[5]], writes=[tWT[jc]])
                fw.op("pe", lambda: nc.tensor.transpose(banks[3][:, 0:128], xsT[:, cs], identF), reads=[txsT, tC], writes=[tB[3]])
                fw.op("pe", lambda: nc.tensor.transpose(banks[3][:, 128:256], BTf[:, cs], identF), reads=[tBTf, tC], acc=[tB[3]])
                fw.op("act", lambda: nc.scalar.activation(out=xtm[jc][:], in_=banks[3][:, 0:128], func=AF.Copy), reads=[tB[3]], writes=[txtm[jc]])
                fw.op("act", lambda: nc.scalar.activation(out=Btm[jc][:], in_=banks[3][:, 128:256], func=AF.Copy), reads=[tB[3]], writes=[tBtm[jc]])
                for h in range(2):
                    hs = slice(h * 64, (h + 1) * 64)
                    fw.op("dve", lambda: nc.vector.tensor_scalar(out=xdt[jc][:, hs], in0=xtm[jc][:, hs], scalar1=Sc[:, h:h + 1], scalar2=None, op0=ALU.mult),
                          reads=[txtm[jc], tS[0]], writes=[txdt[jc]])
                    fw.op("pool", lambda: nc.gpsimd.tensor_scalar(out=xdtd[jc][:, hs], in0=xtm[jc][:, hs], scalar1=Sc[:, 8 + h:9 + h], scalar2=None, op0=ALU.mult),
                          reads=[txtm[jc], tS[4]], writes=[txdtd[jc]])
                for h in range(2):
                    hs = slice(h * 64, (h + 1) * 64)
                    fw.op("pe", lambda: nc.tensor.matmul(banks[6][:, hs], WT[jc][:, h, :], xdt[jc][:, hs], start=True, stop=True),
                          reads=[tWT[jc], txdt[jc]], **({"writes": [tB[6]]} if h == 0 else {"acc": [tB[6]]}))
                fw.op("pe", lambda: nc.tensor.matmul(banks[7][:, 128:256], CTb[:, cs], Hb[:], start=True, stop=True), reads=[tCTb, tHb], writes=[tB[7]])
                for h in range(2):
                    hs = slice(h * 64, (h + 1) * 64)
                    fw.op("act", lambda: nc.scalar.activation(out=yo[jc][:, hs], in_=banks[7][:, 128 + h * 64:128 + (h + 1) * 64], func=AF.Copy, scale=Sc[:, 14 + h:15 + h]),
                          reads=[tB[7], tS[7]], writes=[tyo[jc]])
                fw.op("pe", lambda: nc.tensor.matmul(banks[7][:, 0:128], Btm[jc][:], xdtd[jc][:], start=True, stop=True),
                      reads=[tBtm[jc], txdtd[jc]], acc=[tB[7]])
                for h in range(2):
                    hs = slice(h * 64, (h + 1) * 64)
                    fw.op("dve", lambda: nc.vector.scalar_tensor_tensor(out=H[:, hs], in0=H[:, hs], scalar=Sc[:, 10 + h:11 + h], in1=banks[7][:, hs],
                                                                        op0=ALU.mult, op1=ALU.add),
                          reads=[tB[7], tS[5]], writes=[tH])
                fw.op("act", lambda: nc.scalar.activation(out=Hb[:], in_=H[:], func=AF.Copy), reads=[tH], writes=[tHb])
                for h in range(2):
                    hs = slice(h * 64, (h + 1) * 64)
                    fw.op("dve", lambda: nc.vector.scalar_tensor_tensor(out=yy[jc][:, hs], in0=xtm[jc][:, hs], scalar=V[:, VO_DSK + h:VO_DSK + h + 1],
                                                                        in1=banks[6][:, hs], op0=ALU.mult, op1=ALU.add),
                          reads=[tB[6], txtm[jc], tV], writes=[tyy[jc]])
                fw.op("pool", lambda: nc.gpsimd.tensor_tensor(out=yy[jc][:], in0=yy[jc][:], in1=yo[jc][:], op=ALU.add),
                      reads=[tyo[jc]], writes=[tyy[jc]])
                fw.op("pool", lambda: nc.gpsimd.tensor_tensor(out=hout[j][:, cc, :], in0=yy[jc][:], in1=zs[jc][:], op=ALU.mult),
                      reads=[tyy[jc], tzs[jc]], writes=[thout[j]])
            fw.dma("sp", hss_v[:, ti * 4:(ti + 1) * 4, :], hout[j][:], reads=[thout[j]])
        fw.finish("sp")
        pass
    return nc


from concourse.bass_utils import run_bass_kernel_spmd

SEQ = 16384
BATCH = 2
NT_CORE = 4096
TW_T = 256
POOL_W = (2, 4, 8, 16)


def _consts():
    C = np.zeros((128, NCON), np.float32)
    s = np.arange(128)[:, None]
    l = np.arange(128)[None, :]
    C[:, C_TRI:C_TRI + 128] = (s <= l)
    C[:, C_ONE:C_ONE + 128] = 1
    C[:, C_ID:C_ID + 128] = np.eye(128)
    C[64, C_SEL:C_SEL + 64] = 1
    t = np.arange(512)[None, :]
    for j in range(4):
        C[:, C_NM + 512 * j:C_NM + 512 * (j + 1)] = np.where(128 * j + s > t, NEG, 0.0)
    return C


def _e_inputs(xTb, w_in, b_in, mlnorm, g, consts):
    r = lambda a, n: list(range(a, a + n))
    idx = np.array(r(128 * g, 128) + r(512 + 128 * g, 128) + r(1024 + 128 * g, 128) + r(1536 + 128 * g, 128) + r(3080 + 128 * g, 128)
                   + r(2056 + 128 * g, 128) + r(2568 + 128 * g, 128) + [2048 + g, 2052 + g, 3592 + 2 * g, 3593 + 2 * g])
    ws = np.ascontiguousarray(w_in[:, idx])
    bs = b_in[idx]
    bias_bc = np.ascontiguousarray(np.tile(np.concatenate([bs[128:640], bs[896:900]])[None, :], (128, 1)).astype(np.float32))
    bcol = np.ascontiguousarray(np.stack([bs[0:128], bs[128:256], bs[640:768], bs[768:896]], axis=1).astype(np.float32))
    gml = np.ascontiguousarray(np.tile(mlnorm[128 * g:128 * (g + 1)][None, :], (128, 1)).astype(np.float32))
    return {"xT": xTb, "w_sub": ws, "bias_bc": bias_bc, "bcol": bcol, "gml": gml, "consts": consts}


def _o_inputs(xTb, P, j, consts):
    g = j // 2
    r = np.arange(128)
    idx = np.concatenate([512 + 128 * j + r, 1024 + 128 * g + r, 1280 + 128 * g + r, 1544 + 128 * j + r, 128 * j + r, [1536 + 2 * j, 1537 + 2 * j]])
    ws = np.ascontiguousarray(P['w_in'][:, idx])
    V = np.zeros((128, NVO), np.float32)
    chs = [128 * j + r, 512 + 128 * g + r, 768 + 128 * g + r]
    for gi, ch in enumerate(chs):
        for k in range(4):
            V[:, VO_CW + gi * 4 + k] = P['conv_w'][k, ch]
        V[:, VO_CB + gi] = P['conv_b'][ch]
    for h in range(2):
        V[:, VO_DTB + h] = P['dt_bias'][2 * j + h]
        V[:, VO_ALOG + h] = P['a_log'][2 * j + h]
        V[:, VO_DSK + h] = P['d_skip'][2 * j + h]
    V[:, VO_PB] = P['pool_b'][128 * j + r]
    V[:, VO_PS] = P['pool_scale'][128 * j + r]
    w = POOL_W[j]
    V[:, VO_SELW + j] = 1.0 / w
    V[:, VO_CORR:VO_CORR + 16] = (w / np.minimum(np.arange(16) + 1, w))[None, :]
    return {"xT": xTb, "w_sub": ws, "vecs": V, "wgrp": np.ascontiguousarray(P['pool_w'][j]), "consts": consts}


def _t_vecs(ln1g, ln1b, ln2g, ln2b, cw, cb, hmask, sn):
    V = np.zeros((128, NV), np.float32)
    V[:, V_LN1G:V_LN1G + 8] = ln1g.reshape(8, 128).T
    V[:, V_LN1B:V_LN1B + 8] = ln1b.reshape(8, 128).T
    V[:, V_LN2G:V_LN2G + 8] = ln2g.reshape(8, 128).T
    V[:, V_LN2B:V_LN2B + 8] = ln2b.reshape(8, 128).T
    V[:, V_CW:V_CW + 132] = cw.reshape(3, 44, 128).transpose(2, 0, 1).reshape(128, 132)
    V[:, V_CB:V_CB + 44] = cb.reshape(44, 128).T
    V[:, V_HM] = hmask
    if sn is not None:
        V[:, V_SN:V_SN + 4] = sn.reshape(4, 128).T
    return V


def _halo_slice(aT, q):
    out = np.zeros((aT.shape[0], NT_CORE + 2), np.float32)
    lo = q * NT_CORE - 2
    if lo < 0:
        out[:, 2:] = aT[:, 0:NT_CORE]
    else:
        out[:] = aT[:, lo:lo + NT_CORE + 2]
    return out


def kernel(x, ev_w_in, ev_b_in, ev_ml_norm, ev_w_out, od_w_in, od_conv_w, od_conv_b, od_dt_bias, od_a_log, od_d_skip,
           od_ssm_norm, od_pool_w, od_pool_b, od_pool_scale, od_w_out, ffn_w_up, ffn_conv_w, ffn_conv_b, ffn_w_down,
           ln1_g, ln1_b, ln2_g, ln2_b):
    A = lambda a: np.asarray(a, dtype=np.float32)
    x = A(x)
    consts = _consts()
    cores = list(range(8))
    xT = [np.ascontiguousarray(x[b].T) for b in range(BATCH)]
    for layer in range(4):
        jl = layer // 2
        odd = layer % 2
        hcT = [np.zeros((1024, SEQ), np.float32) for _ in range(BATCH)]
        if not odd:
            nc = build_E(SEQ)
            ims = [_e_inputs(xT[c // 4], A(ev_w_in[jl]), A(ev_b_in[jl]), A(ev_ml_norm[jl]), c % 4, consts) for c in cores]
            res = run_bass_kernel_spmd(nc, ims, core_ids=cores).results
            for c in cores:
                b, g = c // 4, c % 4
                hcT[b][128 * g:128 * (g + 1), :] = res[c]["hml"].T
                hcT[b][512 + 128 * g:512 + 128 * (g + 1), :] = res[c]["hfx"]
            w_out = A(ev_w_out[jl])
            sn = None
        else:
            P = {'w_in': A(od_w_in[jl]), 'conv_w': A(od_conv_w[jl]), 'conv_b': A(od_conv_b[jl]), 'dt_bias': A(od_dt_bias[jl]),
                 'a_log': A(od_a_log[jl]), 'd_skip': A(od_d_skip[jl]), 'pool_w': A(od_pool_w[jl]), 'pool_b': A(od_pool_b[jl]),
                 'pool_scale': A(od_pool_scale[jl])}
            nc = build_O(SEQ)
            ims = [_o_inputs(xT[c // 4], P, c % 4, consts) for c in cores]
            res = run_bass_kernel_spmd(nc, ims, core_ids=cores).results
            for c in cores:
                b, j = c // 4, c % 4
                hcT[b][128 * j:128 * (j + 1), :] = res[c]["hss"].T
                hcT[b][512 + 128 * j:512 + 128 * (j + 1), :] = res[c]["hpl"]
            w_out = A(od_w_out[jl])
            sn = A(od_ssm_norm[jl])
        del res
        nc = build_T(NT_CORE, TW_T, odd)
        ims = []
        for c in cores:
            b, q = c // 4, c % 4
            ims.append({"hcT": _halo_slice(hcT[b], q), "xT": _halo_slice(xT[b], q), "w_out": w_out,
                        "w_up": A(ffn_w_up[layer]), "w_down": A(ffn_w_down[layer]),
                        "vecs": _t_vecs(A(ln1_g[layer]), A(ln1_b[layer]), A(ln2_g[layer]), A(ln2_b[layer]), A(ffn_conv_w[layer]),
                                        A(ffn_conv_b[layer]), 0.0 if q == 0 else 1.0, sn)})
        res = run_bass_kernel_spmd(nc, ims, core_ids=cores).results
        xT = [np.zeros((1024, SEQ), np.float32) for _ in range(BATCH)]
        for c in cores:
            b, q = c // 4, c % 4
            xT[b][:, q * NT_CORE:(q + 1) * NT_CORE] = res[c]["x2T"]
        del res
    out = np.stack([np.ascontiguousarray(xT[b].T) for b in range(BATCH)], axis=0)
    return out.astype(np.float32)
```

```python
import contextlib
import numpy as np
import concourse.bass as bass
import concourse.mybir as mybir

F32 = mybir.dt.float32
BF16 = mybir.dt.bfloat16
AF = mybir.ActivationFunctionType
ALU = mybir.AluOpType
EPOCH = 24000


class Tok:
    __slots__ = ("w", "r", "dsem", "dcnt", "name")

    def __init__(self, name=""):
        self.w = None
        self.r = []
        self.dsem = None
        self.dcnt = 0
        self.name = name


class FW:
    def __init__(self, nc, stack):
        self.nc = nc
        self.stack = stack
        self.eng = {"pe": nc.tensor, "dve": nc.vector, "act": nc.scalar, "pool": nc.gpsimd, "sp": nc.sync}
        self.sems = {}
        self.cnt = {e: 0 for e in self.eng}
        self.seen = {e: {} for e in self.eng}
        self.dma_sems = []
        self.nsem = 0
        self.n_instr = 0

    def _sem(self, key):
        if key not in self.sems:
            self.sems[key] = self.stack.enter_context(self.nc.semaphore(f"s{self.nsem}"))
            self.nsem += 1
        return self.sems[key]

    def _engkey(self, e, n):
        ep = (n - 1) // EPOCH
        return (e, ep), n - ep * EPOCH

    def tok(self, name=""):
        return Tok(name)

    def toks(self, n, name=""):
        return [Tok(f"{name}{i}") for i in range(n)]

    def _wait(self, e, deps):
        engobj = self.eng[e]
        best = {}
        for d in deps:
            if d is None:
                continue
            k, v = d
            if v > best.get(k, 0):
                best[k] = v
        for k, v in best.items():
            if v > self.seen[e].get(k, 0):
                engobj.wait_ge(self._sem(k), v)
                self.seen[e][k] = v

    def op(self, e, fn, reads=(), writes=(), acc=()):
        deps = []
        for t in reads:
            deps.append(t.w)
        for t in writes:
            deps.append(t.w)
            deps.extend(t.r)
        for t in acc:
            if t.w is not None and not (isinstance(t.w[0], tuple) and t.w[0][0] == e):
                deps.append(t.w)
            deps.extend(t.r)
        self._wait(e, deps)
        ins = fn()
        self.cnt[e] += 1
        self.n_instr += 1
        k, v = self._engkey(e, self.cnt[e])
        ins.then_inc(self._sem(k), 1)
        rec = (k, v)
        for t in reads:
            t.r.append(rec)
            if len(t.r) > 24:
                t.r = t.r[-24:] if False else self._compact(t.r)
        for t in list(writes) + list(acc):
            t.w = rec
            t.r = []
        return ins

    def _compact(self, recs):
        best = {}
        for k, v in recs:
            if v > best.get(k, 0):
                best[k] = v
        return list(best.items())

    def dma(self, q, out, in_, reads=(), writes=(), semtok=None, **kw):
        if semtok is None:
            semtok = writes[0] if writes else reads[0]
        if semtok.dsem is None:
            semtok.dsem = ("dma", len(self.dma_sems))
            self.dma_sems.append((semtok.dsem, semtok))
        deps = []
        for t in reads:
            deps.append(t.w)
        for t in writes:
            deps.append(t.w)
            deps.extend(t.r)
        if semtok.dcnt > 0:
            deps.append((semtok.dsem, semtok.dcnt))
        self._wait(q, deps)
        ins = self.eng[q].dma_start(out=out, in_=in_, **kw)
        semtok.dcnt += 16
        ins.then_inc(self._sem(semtok.dsem), 16)
        rec = (semtok.dsem, semtok.dcnt)
        self.n_instr += 1
        for t in reads:
            t.r.append(rec)
        for t in writes:
            t.w = rec
            t.r = []
        return ins

    def barrier(self, engines=("pe", "dve", "act", "pool", "sp")):
        deps = [(k, t.dcnt) for k, t in self.dma_sems if t.dcnt > 0]
        for e2 in self.eng:
            if self.cnt[e2] > 0:
                deps.append(self._engkey(e2, self.cnt[e2]))
        for e in engines:
            self._wait(e, deps)

    def finish(self, e="sp"):
        deps = [(k, t.dcnt) for k, t in self.dma_sems if t.dcnt > 0]
        self._wait(e, deps)


D = 1024
DFF = 2816
NCH = 8
NFC = 22
ALPHA = (2.0 * 4) ** 0.25
LN_EPS = 1e-5
V_LN1G, V_LN1B, V_LN2G, V_LN2B = 0, 8, 16, 24
V_CW = 32
V_CB = 32 + 132
V_HM = V_CB + 44
V_SN = V_HM + 1
NV = V_SN + 4


def build_T(NT, TW, odd):
    NTP = NT + 2
    W2 = TW + 2
    ntiles = (NT + TW - 1) // TW
    nc = bass.Bass("TRN2", target_bir_lowering=False)
    hcT = nc.dram_tensor("hcT", [D, NTP], F32, kind="ExternalInput").ap()
    xT = nc.dram_tensor("xT", [D, NTP], F32, kind="ExternalInput").ap()
    w_out = nc.dram_tensor("w_out", [D, D], F32, kind="ExternalInput").ap()
    w_up = nc.dram_tensor("w_up", [D, 2 * DFF], F32, kind="ExternalInput").ap()
    w_down = nc.dram_tensor("w_down", [DFF, D], F32, kind="ExternalInput").ap()
    vecs = nc.dram_tensor("vecs", [128, NV], F32, kind="ExternalInput").ap()
    x2T = nc.dram_tensor("x2T", [D, NT], F32, kind="ExternalOutput").ap()
    x1T = nc.dram_tensor("x1T_scr", [D, NTP], F32, kind="Internal").ap()

    hcT_v = hcT.rearrange("(k p) n -> p k n", p=128)
    xT_v = xT.rearrange("(k p) n -> p k n", p=128)
    x1T_v = x1T.rearrange("(k p) n -> p k n", p=128)
    x2T_v = x2T.rearrange("(k p) n -> p k n", p=128)

    with contextlib.ExitStack() as st:
        fw = FW(nc, st)
        sb = lambda name, shape, dt: st.enter_context(nc.sbuf_tensor(name, shape, dt))
        V = sb("vecs_sb", [128, NV], F32)
        tV = fw.tok("V")
        fw.dma("sp", V[:], vecs, writes=[tV])
        onesm = sb("onesm", [128, 128], BF16)
        tO = fw.tok("ones")
        fw.op("dve", lambda: nc.vector.memset(onesm[:], 1.0 / D), writes=[tO])
        wup = sb("wup", [128, NCH, 2 * DFF], BF16)
        wdn = sb("wdn", [128, NFC, D], BF16)
        tWup, tWdn = fw.toks(NCH, "wup"), fw.tok("wdn")
        banks = [st.enter_context(nc.psum_tensor(f"bank{i}", [128, 512], F32)) for i in range(8)]
        tB = fw.toks(8, "bank")

        def ln_block(src, ncol, gcol, bcol, tsrc, stat_banks, tmp, dst_f=None, dst_b=None, tdst_f=None, tdst_b=None,
                     col0=0):
            bm, bq = stat_banks
            rb, rq, t1, t2, m2, var, sd, rstd, nmr = tmp["rb"], tmp["rq"], tmp["t1"], tmp["t2"], tmp["m2"], tmp["var"], tmp["sd"], tmp["rstd"], tmp["nmr"]
            for m in range(NCH):
                j = m % 2
                s_ap = src[:, m, col0:col0 + ncol]
                fw.op("act", lambda: nc.scalar.activation(out=rb[j][:, :ncol], in_=s_ap, func=AF.Copy),
                      reads=[tsrc[m]], writes=[tmp["trb"][j]])
                fw.op("act", lambda: nc.scalar.activation(out=rq[j][:, :ncol], in_=s_ap, func=AF.Square),
                      reads=[tsrc[m]], writes=[tmp["trq"][j]])
                fw.op("pe", lambda: nc.tensor.matmul(banks[bm][:, :ncol], onesm[:], rb[j][:, :ncol], start=(m == 0), stop=(m == NCH - 1)),
                      reads=[tO, tmp["trb"][j]], **({"writes": [tB[bm]]} if m == 0 else {"acc": [tB[bm]]}))
                fw.op("pe", lambda: nc.tensor.matmul(banks[bq][:, :ncol], onesm[:], rq[j][:, :ncol], start=(m == 0), stop=(m == NCH - 1)),
                      reads=[tO, tmp["trq"][j]], **({"writes": [tB[bq]]} if m == 0 else {"acc": [tB[bq]]}))
            ts = tmp["tstat"]
            fw.op("act", lambda: nc.scalar.activation(out=m2[:, :ncol], in_=banks[bm][:, :ncol], func=AF.Square),
                  reads=[tB[bm]], writes=[ts[0]])
            fw.op("dve", lambda: nc.vector.tensor_tensor(out=var[:, :ncol], in0=banks[bq][:, :ncol], in1=m2[:, :ncol], op=ALU.subtract),
                  reads=[tB[bq], ts[0]], writes=[ts[1]])
            fw.op("act", lambda: nc.scalar.activation(out=sd[:, :ncol], in_=var[:, :ncol], func=AF.Sqrt, bias=tmp["eps"][:, 0:1]),
                  reads=[ts[1], tmp["teps"]], writes=[ts[2]])
            fw.op("dve", lambda: nc.vector.reciprocal(out=rstd[:, :ncol], in_=sd[:, :ncol]), reads=[ts[2]], writes=[ts[3]])
            fw.op("dve", lambda: nc.vector.scalar_tensor_tensor(out=nmr[:, :ncol], in0=banks[bm][:, :ncol], scalar=-1.0, in1=rstd[:, :ncol],
                                                                op0=ALU.mult, op1=ALU.mult),
                  reads=[tB[bm], ts[3]], writes=[ts[4]])
            for m in range(NCH):
                j = m % 2
                s_ap = src[:, m, col0:col0 + ncol]
                fw.op("pool", lambda: nc.gpsimd.tensor_tensor(out=t1[j][:, :ncol], in0=s_ap, in1=rstd[:, :ncol], op=ALU.mult),
                      reads=[tsrc[m], ts[3]], writes=[tmp["tt1"][j]])
                fw.op("dve", lambda: nc.vector.tensor_tensor(out=t2[j][:, :ncol], in0=t1[j][:, :ncol], in1=nmr[:, :ncol], op=ALU.add),
                      reads=[tmp["tt1"][j], ts[4]], writes=[tmp["tt2"][j]])
                if dst_f is not None:
                    d_ap = dst_f[:, m, 0:ncol]
                    fw.op("act", lambda: nc.scalar.activation(out=d_ap, in_=t2[j][:, :ncol], func=AF.Identity,
                                                              scale=V[:, gcol + m:gcol + m + 1], bias=V[:, bcol + m:bcol + m + 1]),
                          reads=[tmp["tt2"][j], tV], writes=[tdst_f[m]])
                if dst_b is not None:
                    d_ap = dst_b[:, m, 0:ncol]
                    fw.op("act", lambda: nc.scalar.activation(out=d_ap, in_=t2[j][:, :ncol], func=AF.Identity,
                                                              scale=V[:, gcol + m:gcol + m + 1], bias=V[:, bcol + m:bcol + m + 1]),
                          reads=[tmp["tt2"][j], tV], writes=[tdst_b[m]])

        def mk_tmp(stk):
            s2 = lambda name, shape, dt: stk.enter_context(nc.sbuf_tensor(name, shape, dt))
            tmp = {}
            tmp["rb"] = [s2(f"rb{j}", [128, W2], BF16) for j in range(2)]
            tmp["rq"] = [s2(f"rq{j}", [128, W2], BF16) for j in range(2)]
            tmp["t1"] = [s2(f"t1{j}", [128, W2], F32) for j in range(2)]
            tmp["t2"] = [s2(f"t2{j}", [128, W2], F32) for j in range(2)]
            for nm in ["m2", "var", "sd", "rstd", "nmr"]:
                tmp[nm] = s2(nm, [128, W2], F32)
            tmp["eps"] = s2("eps", [128, 1], F32)
            tmp["teps"] = fw.tok()
            fw.op("dve", lambda: nc.vector.memset(tmp["eps"][:], LN_EPS), writes=[tmp["teps"]])
            tmp["trb"], tmp["trq"], tmp["tt1"], tmp["tt2"] = fw.toks(2), fw.toks(2), fw.toks(2), fw.toks(2)
            tmp["tstat"] = fw.toks(5)
            return tmp

        tmp = mk_tmp(st)
        with contextlib.ExitStack() as p1:
            s1 = lambda name, shape, dt: p1.enter_context(nc.sbuf_tensor(name, shape, dt))
            wout = s1("wout", [128, NCH, D], BF16)
            tWout = fw.tok("wout")
            fw.dma("pool", wout[:], w_out.rearrange("(k p) n -> p k n", p=128), writes=[tWout])
            hcb = [s1(f"hcb{j}", [128, NCH, W2], BF16) for j in range(2)]
            xr = [s1(f"xr{j}", [128, NCH, W2], F32) for j in range(2)]
            thc = [fw.toks(NCH) for j in range(2)]
            thcload = fw.toks(2)
            txr = [fw.toks(NCH) for j in range(2)]
            txload = fw.toks(2)
            if odd:
                hcf = [s1(f"hcf{j}", [128, 4, W2], F32) for j in range(2)]
                thcf = fw.toks(2)
                sq = [s1(f"sq{j}", [128, W2], BF16) for j in range(2)]
                tsq = fw.toks(2)
                rs = s1("rs", [128, W2], F32)
                trs = fw.tok()
            for i in range(ntiles):
                j = i % 2
                c0 = i * TW
                wd = min(TW, NT - c0)
                ncol = wd + 2
                if not odd:
                    fw.dma("pool", hcb[j][:, :, :ncol], hcT_v[:, :, c0:c0 + ncol], writes=thc[j], semtok=thcload[j])
                else:
                    fw.dma("pool", hcb[j][:, 4:8, :ncol], hcT_v[:, 4:8, c0:c0 + ncol], writes=thc[j][4:8], semtok=thcload[j])
                    fw.dma("sp", hcf[j][:, :, :ncol], hcT_v[:, 0:4, c0:c0 + ncol], writes=[thcf[j]])
                fw.dma("sp", xr[j][:, :, :ncol], xT_v[:, :, c0:c0 + ncol], writes=txr[j], semtok=txload[j])
                if i == 0:
                    for kk in range(NCH):
                        fw.dma("pool", wup[:, kk, :], w_up[kk * 128:(kk + 1) * 128, :], writes=[tWup[kk]], max_dma_last_dim=8192)
                    fw.dma("pool", wdn[:], w_down.rearrange("(k p) n -> p k n", p=128), writes=[tWdn])
                if odd:
                    for m in range(4):
                        jj = m % 2
                        fw.op("act", lambda: nc.scalar.activation(out=sq[jj][:, :ncol], in_=hcf[j][:, m, :ncol], func=AF.Square),
                              reads=[thcf[j]], writes=[tsq[jj]])
                        fw.op("pe", lambda: nc.tensor.matmul(banks[6][:, :ncol], onesm[:], sq[jj][:, :ncol], start=(m == 0), stop=(m == 3)),
                              reads=[tO, tsq[jj]], **({"writes": [tB[6]]} if m == 0 else {"acc": [tB[6]]}))
                    fw.op("act", lambda: nc.scalar.activation(out=rs[:, :ncol], in_=banks[6][:, :ncol], func=AF.Sqrt, scale=2.0, bias=tmp["eps"][:, 0:1]),
                          reads=[tB[6], tmp["teps"]], writes=[trs])
                    fw.op("dve", lambda: nc.vector.reciprocal(out=rs[:, :ncol], in_=rs[:, :ncol]), reads=[trs], writes=[trs])
                    for m in range(4):
                        fw.op("dve", lambda: nc.vector.scalar_tensor_tensor(out=hcb[j][:, m, :ncol], in0=hcf[j][:, m, :ncol],
                                                                            scalar=V[:, V_SN + m:V_SN + m + 1], in1=rs[:, :ncol],
                                                                            op0=ALU.mult, op1=ALU.mult),
                              reads=[thcf[j], trs, tV], writes=[thc[j][m]])
                for m in range(NCH):
                    bk = m % 4
                    for k in range(NCH):
                        fw.op("pe", lambda: nc.tensor.matmul(banks[bk][:, :ncol], wout[:, k, m * 128:(m + 1) * 128], hcb[j][:, k, :ncol],
                                                             start=(k == 0), stop=(k == NCH - 1)),
                              reads=[tWout, thc[j][k]], **({"writes": [tB[bk]]} if k == 0 else {"acc": [tB[bk]]}))
                    fw.op("dve", lambda: nc.vector.scalar_tensor_tensor(out=xr[j][:, m, :ncol], in0=xr[j][:, m, :ncol], scalar=ALPHA,
                                                                        in1=banks[bk][:, :ncol], op0=ALU.mult, op1=ALU.add),
                          reads=[tB[bk]], writes=[txr[j][m]])
                ln_block(xr[j], ncol, V_LN1G, V_LN1B, txr[j], (4, 5), tmp, dst_f=xr[j], tdst_f=txr[j])
                fw.dma("sp", x1T_v[:, :, c0:c0 + ncol], xr[j][:, :, :ncol], reads=txr[j], semtok=txload[j])
        tX1 = fw.tok("x1scr")
        with contextlib.ExitStack() as p2:
            s2 = lambda name, shape, dt: p2.enter_context(nc.sbuf_tensor(name, shape, dt))
            x1b = [s2(f"x1b{j}", [128, NCH, W2], BF16) for j in range(2)]
            x1f = [s2(f"x1f{j}", [128, NCH, W2], F32) for j in range(2)]
            gT = s2("gT", [128, NFC, TW], BF16)
            tv = [s2(f"tv{j}", [128, TW], F32) for j in range(2)]
            tg = [s2(f"tg{j}", [128, TW], F32) for j in range(2)]
            sg = [s2(f"sg{j}", [128, TW], F32) for j in range(2)]
            ttv, ttg, tsg = fw.toks(2), fw.toks(2), fw.toks(2)
            tx1b = fw.toks(2)
            tx1f = [fw.toks(NCH) for j in range(2)]
            tx1fload = fw.toks(2)
            tgT = fw.toks(NFC)
            fw.barrier()
            for i in range(ntiles):
                j = i % 2
                c0 = i * TW
                wd = min(TW, NT - c0)
                ncol = wd + 2
                fw.dma("pool", x1b[j][:, :, :ncol], x1T_v[:, :, c0:c0 + ncol], writes=[tx1b[j]])
                fw.dma("sp", x1f[j][:, :, :ncol], x1T_v[:, :, c0:c0 + ncol], writes=tx1f[j], semtok=tx1fload[j])
                if i == 0:
                    fw.op("dve", lambda: nc.vector.tensor_scalar(out=x1b[j][:, :, 0:2], in0=x1b[j][:, :, 0:2], scalar1=V[:, V_HM:V_HM + 1],
                                                                 scalar2=None, op0=ALU.mult),
                          reads=[tV], writes=[tx1b[j]])
                for c in range(NFC):
                    jj = c % 2
                    bv, bg = (0, 1) if jj == 0 else (2, 3)
                    for (bk, cc) in ((bv, c), (bg, c + NFC)):
                        for k in range(NCH):
                            fw.op("pe", lambda: nc.tensor.matmul(banks[bk][:, :ncol], wup[:, k, cc * 128:(cc + 1) * 128], x1b[j][:, k, :ncol],
                                                                 start=(k == 0), stop=(k == NCH - 1)),
                                  reads=[tWup[k], tx1b[j]], **({"writes": [tB[bk]]} if k == 0 else {"acc": [tB[bk]]}))
                    for (bk, cc, dst, tdst) in ((bv, c, tv[jj], ttv[jj]), (bg, c + NFC, tg[jj], ttg[jj])):
                        fw.op("act", lambda: nc.scalar.activation(out=dst[:, :wd], in_=banks[bk][:, 2:2 + wd], func=AF.Identity,
                                                                  scale=V[:, V_CW + 2 * 44 + cc:V_CW + 2 * 44 + cc + 1],
                                                                  bias=V[:, V_CB + cc:V_CB + cc + 1]),
                              reads=[tB[bk], tV], writes=[tdst])
                        for kk in (1, 0):
                            fw.op("dve", lambda: nc.vector.scalar_tensor_tensor(out=dst[:, :wd], in0=banks[bk][:, kk:kk + wd],
                                                                                scalar=V[:, V_CW + kk * 44 + cc:V_CW + kk * 44 + cc + 1],
                                                                                in1=dst[:, :wd], op0=ALU.mult, op1=ALU.add),
                                  reads=[tB[bk], tV], writes=[tdst])
                    fw.op("act", lambda: nc.scalar.activation(out=sg[jj][:, :wd], in_=tg[jj][:, :wd], func=AF.Silu),
                          reads=[ttg[jj]], writes=[tsg[jj]])
                    fw.op("pool", lambda: nc.gpsimd.tensor_tensor(out=gT[:, c, :wd], in0=sg[jj][:, :wd], in1=tv[jj][:, :wd], op=ALU.mult),
                          reads=[tsg[jj], ttv[jj]], writes=[tgT[c]])
                for m in range(NCH):
                    bk = 4 + (m % 2)
                    for c in range(NFC):
                        fw.op("pe", lambda: nc.tensor.matmul(banks[bk][:, :wd], wdn[:, c, m * 128:(m + 1) * 128], gT[:, c, :wd],
                                                             start=(c == 0), stop=(c == NFC - 1)),
                              reads=[tWdn, tgT[c]], **({"writes": [tB[bk]]} if c == 0 else {"acc": [tB[bk]]}))
                    fw.op("dve", lambda: nc.vector.scalar_tensor_tensor(out=x1f[j][:, m, 2:2 + wd], in0=x1f[j][:, m, 2:2 + wd], scalar=ALPHA,
                                                                        in1=banks[bk][:, :wd], op0=ALU.mult, op1=ALU.add),
                          reads=[tB[bk]], writes=[tx1f[j][m]])
                ln_block(x1f[j], wd, V_LN2G, V_LN2B, tx1f[j], (6, 7), tmp, dst_f=x1f[j], tdst_f=tx1f[j], col0=2)
                fw.dma("sp", x2T_v[:, :, c0:c0 + wd], x1f[j][:, :, 0:wd], reads=tx1f[j], semtok=tx1fload[j])
        fw.finish("sp")
        pass
    return nc


D = 1024
NCH = 8
NEG = -30000.0
C_TRI = 0
C_ONE = 128
C_ID = 256
C_SEL = 384
C_NM = 448
NCON = C_NM + 4 * 512
W_MQ, W_MK, W_MV, W_MO, W_FV, W_FQ, W_FK, W_G = 0, 128, 256, 384, 512, 640, 768, 896
NW = 900
LN_EPS = 1e-5


def build_E(S):
    TT = 512
    ntile = S // TT
    nblk = S // 128
    nc = bass.Bass("TRN2", target_bir_lowering=False)
    xT = nc.dram_tensor("xT", [D, S], F32, kind="ExternalInput").ap()
    w_sub = nc.dram_tensor("w_sub", [D, NW], F32, kind="ExternalInput").ap()
    bias_bc = nc.dram_tensor("bias_bc", [128, 516], F32, kind="ExternalInput").ap()
    bcol = nc.dram_tensor("bcol", [128, 4], F32, kind="ExternalInput").ap()
    gml = nc.dram_tensor("gml", [128, 128], F32, kind="ExternalInput").ap()
    consts = nc.dram_tensor("consts", [128, NCON], F32, kind="ExternalInput").ap()
    hml = nc.dram_tensor("hml", [S, 128], F32, kind="ExternalOutput").ap()
    hfx = nc.dram_tensor("hfx", [128, S], F32, kind="ExternalOutput").ap()
    qscr = nc.dram_tensor("qscr", [128, S], BF16, kind="Internal").ap()
    kscr = nc.dram_tensor("kscr", [128, S], BF16, kind="Internal").ap()
    fscr = nc.dram_tensor("fscr", [2, 2, S], BF16, kind="Internal").ap()
    v1scr = nc.dram_tensor("v1scr", [2, 128, nblk, 65], BF16, kind="Internal").ap()
    xT_v = xT.rearrange("(k p) n -> p k n", p=128)
    hml_v = hml.rearrange("(c p) d -> p c d", p=128)

    with contextlib.ExitStack() as st:
        fw = FW(nc, st)
        sb = lambda name, shape, dt: st.enter_context(nc.sbuf_tensor(name, shape, dt))
        banks = [st.enter_context(nc.psum_tensor(f"bank{i}", [128, 512], F32)) for i in range(8)]
        tB = fw.toks(8, "bank")
        CF = sb("CF", [128, NCON], F32)
        tC = fw.tok("C")
        fw.dma("sp", CF[:], consts, writes=[tC])
        tri = CF[:, C_TRI:C_TRI + 128]
        onesF = CF[:, C_ONE:C_ONE + 128]
        identF = CF[:, C_ID:C_ID + 128]
        negF_all = sb("negF_all", [128, nblk, 2], F32)
        tNF = fw.toks(nblk, "negF")

        with contextlib.ExitStack() as pa:
            s1 = lambda name, shape, dt: pa.enter_context(nc.sbuf_tensor(name, shape, dt))
            wsb = s1("wsb", [128, NCH, NW], BF16)
            tW = fw.tok("w")
            fw.dma("pool", wsb[:], w_sub.rearrange("(k p) n -> p k n", p=128), writes=[tW])
            bbc = s1("bbc", [128, 516], F32)
            bco = s1("bco", [128, 4], F32)
            gmls = s1("gmls", [128, 128], F32)
            tb = fw.tok("bias")
            fw.dma("sp", bbc[:], bias_bc, writes=[tb], semtok=tb)
            fw.dma("sp", bco[:], bcol, writes=[tb], semtok=tb)
            fw.dma("sp", gmls[:], gml, writes=[tb], semtok=tb)
            xb = [s1(f"xb{j}", [128, NCH, TT], BF16) for j in range(2)]
            txb = fw.toks(2, "xb")
            qT = [s1(f"qT{j}", [128, TT], BF16) for j in range(2)]
            kT = [s1(f"kT{j}", [128, TT], BF16) for j in range(2)]
            fqT = [s1(f"fqT{j}", [128, TT], BF16) for j in range(2)]
            fkT = [s1(f"fkT{j}", [128, TT], BF16) for j in range(2)]
            tqT, tkT, tfqT, tfkT = fw.toks(2), fw.toks(2), fw.toks(2), fw.toks(2)
            tm = [s1(f"tm{j}", [128, 512], F32) for j in range(2)]
            ttm = fw.toks(2)
            g4 = [s1(f"g4{j}", [128, 4], F32) for j in range(2)]
            tg4 = fw.toks(2)
            lsp = [s1(f"lsp{j}", [128, 4], F32) for j in range(2)]
            tlsp = fw.toks(2)
            for j in range(2):
                fw.op("dve", lambda: nc.vector.memset(lsp[j][:], 0.0), writes=[tlsp[j]])
            esc = [s1(f"esc{j}", [128, 8], F32) for j in range(2)]
            tesc = [fw.toks(8) for j in range(2)]
            kw = [s1(f"kw{j}", [128, 128], BF16) for j in range(2)]
            v1 = [s1(f"v1{j}", [128, 129], BF16) for j in range(2)]
            tkw, tv1 = fw.toks(2), fw.toks(2)
            osig = [s1(f"osig{j}", [128, 128], F32) for j in range(2)]
            tosig = fw.toks(2)
            pT = [s1(f"pT{j}", [128, 128], BF16) for j in range(2)]
            tpT = fw.toks(2)
            CN = s1("CN", [128, 129], F32)
            CNb = s1("CNb", [128, 129], BF16)
            tCN, tCNb = fw.tok(), fw.tok()
            fw.op("dve", lambda: nc.vector.memset(CN[:], 0.0), writes=[tCN])
            fw.op("dve", lambda: nc.vector.memset(CNb[:], 0.0), writes=[tCNb])
            for j in range(2):
                fw.op("dve", lambda: nc.vector.memset(v1[j][:, 128:129], 1.0), writes=[tv1[j]])
            v1s = [s1(f"v1s{j}", [128, 2, 4, 65], BF16) for j in range(2)]
            tv1s = fw.toks(2)
            for j in range(2):
                fw.op("dve", lambda: nc.vector.memset(v1s[j][:], 1.0), writes=[tv1s[j]])
            carry = [s1(f"carry{j}", [128, 2], F32) for j in range(2)]
            tcar = fw.toks(2)
            fw.op("dve", lambda: nc.vector.memset(carry[0][:], 0.0), writes=[tcar[0]])
            hh = [s1(f"hh{j}", [128, 128], F32) for j in range(2)]
            thh = fw.toks(2)
            st6 = [s1(f"st6{j}", [128, 6], F32) for j in range(2)]
            mv2 = [s1(f"mv2{j}", [128, 2], F32) for j in range(2)]
            tst6, tmv2 = fw.toks(2), fw.toks(2)
            epsc = s1("epsc", [128, 2], F32)
            tepsc = fw.tok()
            fw.op("dve", lambda: nc.vector.memset(epsc[:, 0:1], LN_EPS), writes=[tepsc])
            fw.op("dve", lambda: nc.vector.memset(epsc[:, 1:2], 1.0), writes=[tepsc])
            hout = [s1(f"hout{j}", [128, 4, 128], F32) for j in range(2)]
            thout = fw.toks(2)
            frow = [s1(f"frow{j}", [2, 2, TT], BF16) for j in range(2)]
            tfrow = fw.toks(2)
            tqs, tks, tfs, tv1scr = fw.tok(), fw.tok(), fw.tok(), fw.tok()

            for ti in range(ntile):
                j = ti % 2
                c0 = ti * TT
                if ti == 0:
                    fw.dma("pool", xb[0][:], xT_v[:, :, 0:TT], writes=[txb[0]])
                if ti + 1 < ntile:
                    fw.dma("pool", xb[1 - j][:], xT_v[:, :, c0 + TT:c0 + 2 * TT], writes=[txb[1 - j]])
                for gi, (wc, dst, tdst, scale) in enumerate(((W_MQ, qT[j], tqT[j], 128 ** -0.5), (W_MK, kT[j], tkT[j], 1.0),
                                                              (W_FQ, fqT[j], tfqT[j], 0.125), (W_FK, fkT[j], tfkT[j], 1.0))):
                    bk = gi % 2
                    for k in range(NCH):
                        fw.op("pe", lambda: nc.tensor.matmul(banks[bk][:, :], wsb[:, k, wc:wc + 128], xb[j][:, k, :], start=(k == 0), stop=(k == NCH - 1)),
                              reads=[tW, txb[j]], **({"writes": [tB[bk]]} if k == 0 else {"acc": [tB[bk]]}))
                    fw.op("dve", lambda: nc.vector.tensor_scalar(out=dst[:], in0=banks[bk][:, :], scalar1=bco[:, gi:gi + 1], scalar2=scale,
                                                                 op0=ALU.add, op1=ALU.mult),
                          reads=[tB[bk], tb], writes=[tdst])
                fw.dma("sp", qscr[:, c0:c0 + TT], fqT[j][:], reads=[tfqT[j]], semtok=tqs)
                fw.dma("sp", kscr[:, c0:c0 + TT], fkT[j][:], reads=[tfkT[j]], semtok=tks)
                for cc in range(4):
                    blk = ti * 4 + cc
                    jc = blk % 2
                    cs = slice(cc * 128, (cc + 1) * 128)
                    bkt = 2 + jc
                    for k in range(NCH):
                        fw.op("pe", lambda: nc.tensor.matmul(banks[bkt][:, :], xb[j][:, k, cs], wsb[:, k, W_MK:W_MK + 512], start=(k == 0), stop=(k == NCH - 1)),
                              reads=[tW, txb[j]], **({"writes": [tB[bkt]]} if k == 0 else {"acc": [tB[bkt]]}))
                    for k in range(NCH):
                        fw.op("pe", lambda: nc.tensor.matmul(banks[4][:, 0:4], xb[j][:, k, cs], wsb[:, k, W_G:W_G + 4], start=(k == 0), stop=(k == NCH - 1)),
                              reads=[tW, txb[j]], **({"writes": [tB[4]]} if k == 0 else {"acc": [tB[4]]}))
                    fw.op("dve", lambda: nc.vector.tensor_tensor(out=tm[jc][:], in0=banks[bkt][:, :], in1=bbc[:, 0:512], op=ALU.add),
                          reads=[tB[bkt], tb], writes=[ttm[jc]])
                    fw.op("dve", lambda: nc.vector.tensor_tensor(out=g4[jc][:], in0=banks[4][:, 0:4], in1=bbc[:, 512:516], op=ALU.add),
                          reads=[tB[4], tb], writes=[tg4[jc]])
                    fw.op("act", lambda: nc.scalar.activation(out=lsp[jc][:, 0:3], in_=g4[jc][:, 1:4], func=AF.Exp, scale=-1.0),
                          reads=[tg4[jc]], writes=[tlsp[jc]])
                    fw.op("act", lambda: nc.scalar.activation(out=lsp[jc][:, 0:3], in_=lsp[jc][:, 0:3], func=AF.Ln, bias=epsc[:, 1:2]),
                          reads=[tepsc], writes=[tlsp[jc]])
                    fw.op("pe", lambda: nc.tensor.matmul(banks[5][:, 0:4], tri, lsp[jc][:], start=True, stop=True),
                          reads=[tC, tlsp[jc]], writes=[tB[5]])
                    fw.op("pe", lambda: nc.tensor.matmul(banks[5][:, 8:12], onesF, lsp[jc][:], start=True, stop=True),
                          reads=[tC, tlsp[jc]], acc=[tB[5]])
                    E = esc[jc]
                    tE = tesc[jc]
                    fw.op("act", lambda: nc.scalar.activation(out=E[:, 0:1], in_=banks[5][:, 0:1], func=AF.Exp, bias=g4[jc][:, 0:1]),
                          reads=[tB[5], tg4[jc]], writes=[tE[0]])
                    fw.op("act", lambda: nc.scalar.activation(out=E[:, 1:2], in_=banks[5][:, 0:1], func=AF.Exp, scale=-1.0),
                          reads=[tB[5]], writes=[tE[1]])
                    fw.op("dve", lambda: nc.vector.tensor_tensor(out=E[:, 4:5], in0=g4[jc][:, 0:1], in1=banks[5][:, 8:9], op=ALU.subtract),
                          reads=[tB[5], tg4[jc]], writes=[tE[4]])
                    fw.op("act", lambda: nc.scalar.activation(out=E[:, 2:3], in_=banks[5][:, 0:1], func=AF.Exp, bias=E[:, 4:5]),
                          reads=[tB[5], tE[4]], writes=[tE[2]])
                    fw.op("act", lambda: nc.scalar.activation(out=E[:, 3:4], in_=banks[5][:, 8:9], func=AF.Exp, scale=-1.0),
                          reads=[tB[5]], writes=[tE[3]])
                    cj, cn = blk % 2, (blk + 1) % 2
                    fw.op("dve", lambda: nc.vector.tensor_tensor(out=negF_all[:, blk, :], in0=banks[5][:, 1:3], in1=carry[cj][:], op=ALU.subtract),
                          reads=[tB[5], tcar[cj]], writes=[tNF[blk]])
                    fw.op("dve", lambda: nc.vector.tensor_tensor(out=carry[cn][:], in0=carry[cj][:], in1=banks[5][:, 9:11], op=ALU.subtract),
                          reads=[tB[5], tcar[cj]], writes=[tcar[cn]])
                    fw.op("dve", lambda: nc.vector.tensor_scalar(out=kw[jc][:], in0=tm[jc][:, 0:128], scalar1=E[:, 2:3], scalar2=None, op0=ALU.mult),
                          reads=[ttm[jc], tE[2]], writes=[tkw[jc]])
                    fw.op("pool", lambda: nc.gpsimd.tensor_copy(out=v1[jc][:, 0:128], in_=tm[jc][:, 128:256]),
                          reads=[ttm[jc]], writes=[tv1[jc]])
                    fw.op("act", lambda: nc.scalar.activation(out=osig[jc][:], in_=tm[jc][:, 256:384], func=AF.Sigmoid),
                          reads=[ttm[jc]], writes=[tosig[jc]])
                    fw.op("pool", lambda: nc.gpsimd.tensor_copy(out=v1s[j][:, :, cc, 0:64], in_=tm[jc][:, 384:512].rearrange("p (h d) -> p h d", h=2)),
                          reads=[ttm[jc]], writes=[tv1s[j]])
                    fw.op("pe", lambda: nc.tensor.matmul(banks[6][:, 0:128], kT[j][:, cs], qT[j][:, cs], start=True, stop=True),
                          reads=[tkT[j], tqT[j]], writes=[tB[6]])
                    fw.op("dve", lambda: nc.vector.scalar_tensor_tensor(out=pT[jc][:], in0=banks[6][:, 0:128], scalar=E[:, 0:1], in1=tri,
                                                                        op0=ALU.mult, op1=ALU.mult),
                          reads=[tB[6], tE[0], tC], writes=[tpT[jc]])
                    fw.op("pe", lambda: nc.tensor.matmul(banks[7][:, 0:129], pT[jc][:], v1[jc][:], start=True, stop=False),
                          reads=[tpT[jc], tv1[jc]], writes=[tB[7]])
                    fw.op("pe", lambda: nc.tensor.matmul(banks[7][:, 0:129], qT[j][:, cs], CNb[:], start=False, stop=True),
                          reads=[tqT[j], tCNb], acc=[tB[7]])
                    fw.op("pe", lambda: nc.tensor.matmul(banks[6][:, 256:385], kw[jc][:], v1[jc][:], start=True, stop=True),
                          reads=[tkw[jc], tv1[jc]], writes=[tB[6]])
                    fw.op("dve", lambda: nc.vector.scalar_tensor_tensor(out=CN[:], in0=CN[:], scalar=E[:, 3:4], in1=banks[6][:, 256:385],
                                                                        op0=ALU.mult, op1=ALU.add),
                          reads=[tB[6], tE[3]], writes=[tCN])
                    fw.op("act", lambda: nc.scalar.activation(out=CNb[:], in_=CN[:], func=AF.Copy), reads=[tCN], writes=[tCNb])
                    fw.op("dve", lambda: nc.vector.tensor_tensor(out=E[:, 5:6], in0=banks[7][:, 128:129], in1=E[:, 1:2], op=ALU.mult),
                          reads=[tB[7], tE[1]], writes=[tE[5]])
                    fw.op("dve", lambda: nc.vector.tensor_scalar(out=E[:, 6:7], in0=E[:, 5:6], scalar1=1.0, scalar2=None, op0=ALU.max),
                          reads=[tE[5]], writes=[tE[6]])
                    fw.op("dve", lambda: nc.vector.scalar_tensor_tensor(out=E[:, 5:6], in0=E[:, 5:6], scalar=-1.0, in1=E[:, 6:7], op0=ALU.mult, op1=ALU.max),
                          reads=[tE[6]], writes=[tE[5]])
                    fw.op("dve", lambda: nc.vector.reciprocal(out=E[:, 5:6], in_=E[:, 5:6]), writes=[tE[5]])
                    fw.op("dve", lambda: nc.vector.tensor_tensor(out=E[:, 6:7], in0=E[:, 5:6], in1=E[:, 1:2], op=ALU.mult),
                          reads=[tE[5], tE[1]], writes=[tE[6]])
                    fw.op("act", lambda: nc.scalar.activation(out=hh[jc][:], in_=banks[7][:, 0:128], func=AF.Copy, scale=E[:, 6:7]),
                          reads=[tB[7], tE[6]], writes=[thh[jc]])
                    fw.op("dve", lambda: nc.vector.bn_stats(out=st6[jc][:], in_=hh[jc][:]), reads=[thh[jc]], writes=[tst6[jc]])
                    fw.op("dve", lambda: nc.vector.bn_aggr(out=mv2[jc][:], in_=st6[jc][:]), reads=[tst6[jc]], writes=[tmv2[jc]])
                    fw.op("act", lambda: nc.scalar.activation(out=E[:, 7:8], in_=mv2[jc][:, 1:2], func=AF.Sqrt, bias=epsc[:, 0:1]),
                          reads=[tmv2[jc], tepsc], writes=[tE[7]])
                    fw.op("dve", lambda: nc.vector.reciprocal(out=E[:, 7:8], in_=E[:, 7:8]), writes=[tE[7]])
                    fw.op("dve", lambda: nc.vector.tensor_scalar(out=hh[jc][:], in0=hh[jc][:], scalar1=mv2[jc][:, 0:1], scalar2=E[:, 7:8],
                                                                 op0=ALU.subtract, op1=ALU.mult),
                          reads=[tmv2[jc], tE[7]], writes=[thh[jc]])
                    fw.op("pool", lambda: nc.gpsimd.tensor_tensor(out=hh[jc][:], in0=hh[jc][:], in1=gmls[:], op=ALU.mult),
                          reads=[tb], writes=[thh[jc]])
                    fw.op("dve", lambda: nc.vector.tensor_tensor(out=hout[j][:, cc, :], in0=hh[jc][:], in1=osig[jc][:], op=ALU.mult),
                          reads=[thh[jc], tosig[jc]], writes=[thout[j]])
                fw.dma("sp", hml_v[:, ti * 4:(ti + 1) * 4, :], hout[j][:], reads=[thout[j]])
                for h in range(2):
                    fw.dma("sp", v1scr[h, :, ti * 4:(ti + 1) * 4, :], v1s[j][:, h, :, :], reads=[tv1s[j]], semtok=tv1scr)
                for cc in range(4):
                    blk = ti * 4 + cc
                    fw.op("pe", lambda: nc.tensor.transpose(banks[0][0:2, cc * 128:(cc + 1) * 128], negF_all[:, blk, :], identF),
                          reads=[tNF[blk], tC], **({"writes": [tB[0]]} if cc == 0 else {"acc": [tB[0]]}))
                fw.op("act", lambda: nc.scalar.activation(out=frow[j][:, 0, :], in_=banks[0][0:2, :], func=AF.Copy, scale=-1.0),
                      reads=[tB[0]], writes=[tfrow[j]])
                fw.op("dve", lambda: nc.vector.scalar_tensor_tensor(out=frow[j][:, 1, :], in0=banks[0][0:2, :], scalar=-1.0, in1=frow[j][:, 0, :],
                                                                    op0=ALU.mult, op1=ALU.subtract),
                      reads=[tB[0]], writes=[tfrow[j]])
                for hl in range(2):
                    fw.dma("sp", fscr[hl, :, c0:c0 + TT], frow[j][:, hl, :], reads=[tfrow[j]], semtok=tfs)
            scr_deps = [(t.dsem, t.dcnt) for t in (tqs, tks, tfs, tv1scr)]
        fw.barrier()
        with contextlib.ExitStack() as pb:
            s2 = lambda name, shape, dt: pb.enter_context(nc.sbuf_tensor(name, shape, dt))
            qa = s2("qa", [66, S], BF16)
            ka = s2("ka", [66, S], BF16)
            V1 = s2("V1", [128, nblk, 65], BF16)
            tqa, tka, tV1 = fw.tok(), fw.tok(), fw.tok()
            nmb = s2("nmb", [128, 4, 512], BF16)
            idb = s2("idb", [128, 128], BF16)
            tnm = fw.tok()
            fw.op("dve", lambda: nc.vector.tensor_copy(out=nmb[:], in_=CF[:, C_NM:C_NM + 2048].rearrange("p (j t) -> p j t", j=4)), reads=[tC], writes=[tnm])
            fw.op("dve", lambda: nc.vector.tensor_copy(out=idb[:], in_=identF), reads=[tC], writes=[tnm])
            PT = [s2(f"PT{j}", [128, 512], BF16) for j in range(3)]
            tPT = fw.toks(3)
            osb = [s2(f"osb{j}", [65, 512], F32) for j in range(2)]
            tosb = fw.toks(2)
            rden = [s2(f"rden{j}", [64, 512], F32) for j in range(2)]
            trden = fw.toks(2)
            hfo = [s2(f"hfo{j}", [64, 512], F32) for j in range(2)]
            thfo = fw.toks(2)
            nq = S // 512
            it = 0
            for h in range(2):
                fw.dma("sp", qa[0:64, :], qscr[h * 64:(h + 1) * 64, :], writes=[tqa], semtok=tqa)
                fw.dma("sp", qa[64:65, :], fscr[0, h:h + 1, :], writes=[tqa], semtok=tqa)
                fw.dma("sp", qa[65:66, :], fscr[1, h:h + 1, :], writes=[tqa], semtok=tqa)
                fw.dma("sp", ka[0:64, :], kscr[h * 64:(h + 1) * 64, :], writes=[tka], semtok=tka)
                fw.op("dve", lambda: nc.vector.memset(ka[64:66, :], 1.0), writes=[tka])
                fw.dma("sp", V1[:], v1scr[h], writes=[tV1])
                for qi in range(nq):
                    jo = qi % 2
                    bo = 3 + jo
                    nkb = 4 * qi + 4
                    qs = slice(qi * 512, (qi + 1) * 512)
                    for kb in range(nkb):
                        js = it % 3
                        it += 1
                        ks = slice(kb * 128, (kb + 1) * 128)
                        diag = kb >= 4 * qi
                        fw.op("pe", lambda: nc.tensor.matmul(banks[js][:, :], ka[:, ks], qa[:, qs], start=True, stop=not diag),
                              reads=[tka, tqa], writes=[tB[js]])
                        if diag:
                            fw.op("pe", lambda: nc.tensor.matmul(banks[js][:, :], idb[:], nmb[:, kb - 4 * qi, :], start=False, stop=True),
                                  reads=[tnm], acc=[tB[js]])
                        fw.op("act", lambda: nc.scalar.activation(out=PT[js][:], in_=banks[js][:, :], func=AF.Exp, bias=negF_all[:, kb, h:h + 1]),
                              reads=[tB[js], tNF[kb]], writes=[tPT[js]])
                        fw.op("pe", lambda: nc.tensor.matmul(banks[bo][0:65, :], V1[:, kb, :], PT[js][:], start=(kb == 0), stop=(kb == nkb - 1)),
                              reads=[tV1, tPT[js]], **({"writes": [tB[bo]]} if kb == 0 else {"acc": [tB[bo]]}))
                    fw.op("act", lambda: nc.scalar.activation(out=osb[jo][:], in_=banks[bo][0:65, :], func=AF.Copy), reads=[tB[bo]], writes=[tosb[jo]])
                    fw.op("pe", lambda: nc.tensor.matmul(banks[5][0:64, :], CF[0:65, C_SEL:C_SEL + 64], osb[jo][:], start=True, stop=True),
                          reads=[tC, tosb[jo]], writes=[tB[5]])
                    fw.op("dve", lambda: nc.vector.reciprocal(out=rden[jo][:], in_=banks[5][0:64, :]), reads=[tB[5]], writes=[trden[jo]])
                    fw.op("dve", lambda: nc.vector.tensor_tensor(out=hfo[jo][:], in0=osb[jo][0:64, :], in1=rden[jo][:], op=ALU.mult),
                          reads=[tosb[jo], trden[jo]], writes=[thfo[jo]])
                    fw.dma("sp", hfx[h * 64:(h + 1) * 64, qs], hfo[jo][:], reads=[thfo[jo]])
        fw.finish("sp")
        pass
    return nc


D = 1024
NCH = 8
NWO = 642
WO_X, WO_B, WO_C, WO_P, WO_Z, WO_DT = 0, 128, 256, 384, 512, 640
VO_CW, VO_CB, VO_DTB, VO_ALOG, VO_DSK, VO_PB, VO_PS, VO_SELW, VO_CORR = 0, 12, 15, 17, 19, 21, 22, 23, 27
NVO = 43
BIG = 1e30


def build_O(S):
    TT = 512
    ntile = S // TT
    nc = bass.Bass("TRN2", target_bir_lowering=False)
    xT = nc.dram_tensor("xT", [D, S], F32, kind="ExternalInput").ap()
    w_sub = nc.dram_tensor("w_sub", [D, NWO], F32, kind="ExternalInput").ap()
    vecs = nc.dram_tensor("vecs", [128, NVO], F32, kind="ExternalInput").ap()
    wgrp = nc.dram_tensor("wgrp", [128, 128], F32, kind="ExternalInput").ap()
    consts = nc.dram_tensor("consts", [128, NCON], F32, kind="ExternalInput").ap()
    hss = nc.dram_tensor("hss", [S, 128], F32, kind="ExternalOutput").ap()
    hpl = nc.dram_tensor("hpl", [128, S], F32, kind="ExternalOutput").ap()
    xT_v = xT.rearrange("(k p) n -> p k n", p=128)
    hss_v = hss.rearrange("(c p) d -> p c d", p=128)

    with contextlib.ExitStack() as st:
        fw = FW(nc, st)
        sb = lambda name, shape, dt: st.enter_context(nc.sbuf_tensor(name, shape, dt))
        banks = [st.enter_context(nc.psum_tensor(f"bank{i}", [128, 512], F32)) for i in range(8)]
        tB = fw.toks(8, "bank")
        CF = sb("CF", [128, NCON], F32)
        tC = fw.tok("C")
        fw.dma("sp", CF[:], consts, writes=[tC])
        tri = CF[:, C_TRI:C_TRI + 128]
        onesF = CF[:, C_ONE:C_ONE + 128]
        identF = CF[:, C_ID:C_ID + 128]
        V = sb("V", [128, NVO], F32)
        tV = fw.tok("V")
        fw.dma("sp", V[:], vecs, writes=[tV])
        wsb = sb("wsb", [128, NCH, NWO], BF16)
        tW = fw.tok("w")
        fw.dma("pool", wsb[:], w_sub.rearrange("(k p) n -> p k n", p=128), writes=[tW])
        wg = sb("wg", [128, 128], BF16)
        tWg = fw.tok("wg")
        fw.dma("pool", wg[:], wgrp, writes=[tWg])
        negA = sb("negA", [128, 2], F32)
        tnA = fw.tok()
        fw.op("act", lambda: nc.scalar.activation(out=negA[:], in_=V[:, VO_ALOG:VO_ALOG + 2], func=AF.Exp), reads=[tV], writes=[tnA])
        posm = sb("posm", [128, 128], F32)
        tposm = fw.tok()
        fw.op("dve", lambda: nc.vector.tensor_scalar(out=posm[:], in0=tri, scalar1=-30000.0, scalar2=30000.0, op0=ALU.mult, op1=ALU.add), reads=[tC], writes=[tposm])
        yo = [sb(f"yo{j}", [128, 128], F32) for j in range(2)]
        tyo = fw.toks(2)
        one1 = sb("one1", [128, 1], F32)
        tone = fw.tok()
        fw.op("dve", lambda: nc.vector.memset(one1[:], 1.0), writes=[tone])

        xb = [sb(f"xb{j}", [128, NCH, TT], BF16) for j in range(2)]
        txb = fw.toks(2)
        ubuf = [sb(f"ubuf{g}", [128, 3 + TT], F32) for g in range(3)]
        tub = fw.toks(3)
        for g in range(3):
            fw.op("dve", lambda: nc.vector.memset(ubuf[g][:, 0:3], 0.0), writes=[tub[g]])
        cacc = [sb(f"cacc{g}", [128, TT], F32) for g in range(3)]
        tcacc = fw.toks(3)
        xsT = sb("xsT", [128, TT], F32)
        BTf = sb("BTf", [128, TT], F32)
        BTb = sb("BTb", [128, TT], BF16)
        CTb = sb("CTb", [128, TT], BF16)
        txsT, tBTf, tBTb, tCTb = fw.tok(), fw.tok(), fw.tok(), fw.tok()
        pbuf = sb("pbuf", [128, 15 + TT], F32)
        tpb = fw.tok()
        fw.op("dve", lambda: nc.vector.memset(pbuf[:, 0:15], 0.0), writes=[tpb])
        sl = [sb(f"sl{i}", [128, 15 + TT], F32) for i in range(4)]
        tsl = fw.toks(4)
        pacc = sb("pacc", [128, TT], F32)
        tpacc = fw.tok()
        pooled = sb("pooled", [128, TT], BF16)
        tpooled = fw.tok()
        pout = [sb(f"pout{j}", [128, TT], F32) for j in range(2)]
        tpout = fw.toks(2)
        zd = [sb(f"zd{j}", [128, 130], F32) for j in range(2)]
        tzd = fw.toks(2)
        zs = [sb(f"zs{j}", [128, 128], F32) for j in range(2)]
        tzs = fw.toks(2)
        sc = [sb(f"sc{j}", [128, 16], F32) for j in range(2)]
        tsc = [fw.toks(8) for j in range(2)]
        labc = [sb(f"labc{j}", [128, 2, 128], F32) for j in range(2)]
        tlabc = fw.toks(2)
        xtm = [sb(f"xtm{j}", [128, 128], F32) for j in range(2)]
        Btm = [sb(f"Btm{j}", [128, 128], BF16) for j in range(2)]
        txtm, tBtm = fw.toks(2), fw.toks(2)
        xdt = [sb(f"xdt{j}", [128, 128], BF16) for j in range(2)]
        xdtd = [sb(f"xdtd{j}", [128, 128], BF16) for j in range(2)]
        txdt, txdtd = fw.toks(2), fw.toks(2)
        Gm = [sb(f"Gm{j}", [128, 128], F32) for j in range(2)]
        tGm = fw.toks(2)
        Lr = [sb(f"Lr{j}", [128, 2, 128], F32) for j in range(2)]
        tLr = fw.toks(2)
        WT = [sb(f"WT{j}", [128, 2, 128], BF16) for j in range(2)]
        tWT = fw.toks(2)
        ECs = [sb(f"ECs{j}", [128, 2, 128], F32) for j in range(2)]
        tECs = fw.toks(2)
        CTs = [sb(f"CTs{j}", [128, 2, 128], BF16) for j in range(2)]
        tCTs = fw.toks(2)
        H = sb("H", [128, 128], F32)
        Hb = sb("Hb", [128, 128], BF16)
        tH, tHb = fw.tok(), fw.tok()
        fw.op("dve", lambda: nc.vector.memset(H[:], 0.0), writes=[tH])
        fw.op("dve", lambda: nc.vector.memset(Hb[:], 0.0), writes=[tHb])
        yy = [sb(f"yy{j}", [128, 128], F32) for j in range(2)]
        tyy = fw.toks(2)
        hout = [sb(f"hout{j}", [128, 4, 128], F32) for j in range(2)]
        thout = fw.toks(2)

        for ti in range(ntile):
            j = ti % 2
            c0 = ti * TT
            if ti == 0:
                fw.dma("pool", xb[0][:], xT_v[:, :, 0:TT], writes=[txb[0]])
            if ti + 1 < ntile:
                fw.dma("pool", xb[1 - j][:], xT_v[:, :, c0 + TT:c0 + 2 * TT], writes=[txb[1 - j]])
            for g, wc in enumerate((WO_X, WO_B, WO_C)):
                bk = g % 2
                for k in range(NCH):
                    fw.op("pe", lambda: nc.tensor.matmul(banks[bk][:, :], wsb[:, k, wc:wc + 128], xb[j][:, k, :], start=(k == 0), stop=(k == NCH - 1)),
                          reads=[tW, txb[j]], **({"writes": [tB[bk]]} if k == 0 else {"acc": [tB[bk]]}))
                fw.op("act", lambda: nc.scalar.activation(out=ubuf[g][:, 3:3 + TT], in_=banks[bk][:, :], func=AF.Copy), reads=[tB[bk]], writes=[tub[g]])
                cw = lambda k: V[:, VO_CW + g * 4 + k:VO_CW + g * 4 + k + 1]
                fw.op("dve", lambda: nc.vector.tensor_scalar(out=cacc[g][:], in0=ubuf[g][:, 3:3 + TT], scalar1=cw(3), scalar2=V[:, VO_CB + g:VO_CB + g + 1],
                                                             op0=ALU.mult, op1=ALU.add),
                      reads=[tub[g], tV], writes=[tcacc[g]])
                for k in (2, 1, 0):
                    fw.op("dve", lambda: nc.vector.scalar_tensor_tensor(out=cacc[g][:], in0=ubuf[g][:, k:k + TT], scalar=cw(k), in1=cacc[g][:],
                                                                        op0=ALU.mult, op1=ALU.add),
                          reads=[tub[g], tV], writes=[tcacc[g]])
                fw.op("pool", lambda: nc.gpsimd.tensor_copy(out=ubuf[g][:, 0:3], in_=ubuf[g][:, TT:TT + 3]), reads=[], writes=[tub[g]])
                if g == 0:
                    fw.op("act", lambda: nc.scalar.activation(out=xsT[:], in_=cacc[g][:], func=AF.Silu), reads=[tcacc[g]], writes=[txsT])
                elif g == 1:
                    fw.op("act", lambda: nc.scalar.activation(out=BTf[:], in_=cacc[g][:], func=AF.Silu), reads=[tcacc[g]], writes=[tBTf])
                    fw.op("pool", lambda: nc.gpsimd.tensor_copy(out=BTb[:], in_=BTf[:]), reads=[tBTf], writes=[tBTb])
                else:
                    fw.op("act", lambda: nc.scalar.activation(out=CTb[:], in_=cacc[g][:], func=AF.Silu), reads=[tcacc[g]], writes=[tCTb])
            bk = 1
            for k in range(NCH):
                fw.op("pe", lambda: nc.tensor.matmul(banks[bk][:, :], wsb[:, k, WO_P:WO_P + 128], xb[j][:, k, :], start=(k == 0), stop=(k == NCH - 1)),
                      reads=[tW, txb[j]], **({"writes": [tB[bk]]} if k == 0 else {"acc": [tB[bk]]}))
            fw.op("act", lambda: nc.scalar.activation(out=pbuf[:, 15:15 + TT], in_=banks[bk][:, :], func=AF.Copy), reads=[tB[bk]], writes=[tpb])
            prev, tprev = pbuf, tpb
            for i, sh in enumerate((1, 2, 4, 8)):
                lo = 2 * sh - 1
                fw.op("pool", lambda: nc.gpsimd.tensor_tensor(out=sl[i][:, lo:15 + TT], in0=prev[:, lo:15 + TT], in1=prev[:, lo - sh:15 + TT - sh], op=ALU.add),
                      reads=[tprev], writes=[tsl[i]])
                prev, tprev = sl[i], tsl[i]
            fw.op("dve", lambda: nc.vector.tensor_scalar(out=pacc[:], in0=sl[0][:, 15:15 + TT], scalar1=V[:, VO_SELW:VO_SELW + 1], scalar2=None, op0=ALU.mult),
                  reads=[tsl[0], tV], writes=[tpacc])
            for i in (1, 2, 3):
                fw.op("dve", lambda: nc.vector.scalar_tensor_tensor(out=pacc[:], in0=sl[i][:, 15:15 + TT], scalar=V[:, VO_SELW + i:VO_SELW + i + 1], in1=pacc[:],
                                                                    op0=ALU.mult, op1=ALU.add),
                      reads=[tsl[i], tV], writes=[tpacc])
            if ti == 0:
                fw.op("dve", lambda: nc.vector.tensor_tensor(out=pacc[:, 0:16], in0=pacc[:, 0:16], in1=V[:, VO_CORR:VO_CORR + 16], op=ALU.mult),
                      reads=[tV], writes=[tpacc])
            fw.op("dve", lambda: nc.vector.tensor_tensor(out=pooled[:], in0=pacc[:], in1=pbuf[:, 15:15 + TT], op=ALU.subtract),
                  reads=[tpacc, tpb], writes=[tpooled])
            fw.op("pool", lambda: nc.gpsimd.tensor_copy(out=pbuf[:, 0:15], in_=pbuf[:, TT:TT + 15]), reads=[], writes=[tpb])
            fw.op("pe", lambda: nc.tensor.matmul(banks[0][:, :], wg[:], pooled[:], start=True, stop=True), reads=[tWg, tpooled], writes=[tB[0]])
            fw.op("dve", lambda: nc.vector.tensor_scalar(out=pout[j][:], in0=banks[0][:, :], scalar1=V[:, VO_PB:VO_PB + 1], scalar2=V[:, VO_PS:VO_PS + 1],
                                                         op0=ALU.add, op1=ALU.mult),
                  reads=[tB[0], tV], writes=[tpout[j]])
            fw.dma("sp", hpl[:, c0:c0 + TT], pout[j][:], reads=[tpout[j]])
            for cc in range(4):
                blk = ti * 4 + cc
                jc = blk % 2
                cs = slice(cc * 128, (cc + 1) * 128)
                Sc, tS = sc[jc], tsc[jc]
                for k in range(NCH):
                    fw.op("pe", lambda: nc.tensor.matmul(banks[2][:, 0:130], xb[j][:, k, cs], wsb[:, k, WO_Z:WO_Z + 130], start=(k == 0), stop=(k == NCH - 1)),
                          reads=[tW, txb[j]], **({"writes": [tB[2]]} if k == 0 else {"acc": [tB[2]]}))
                fw.op("act", lambda: nc.scalar.activation(out=zd[jc][:], in_=banks[2][:, 0:130], func=AF.Copy), reads=[tB[2]], writes=[tzd[jc]])
                fw.op("act", lambda: nc.scalar.activation(out=zs[jc][:], in_=zd[jc][:, 0:128], func=AF.Silu), reads=[tzd[jc]], writes=[tzs[jc]])
                fw.op("dve", lambda: nc.vector.tensor_tensor(out=Sc[:, 0:2], in0=zd[jc][:, 128:130], in1=V[:, VO_DTB:VO_DTB + 2], op=ALU.add),
                      reads=[tzd[jc], tV], writes=[tS[0]])
                fw.op("act", lambda: nc.scalar.activation(out=Sc[:, 0:2], in_=Sc[:, 0:2], func=AF.Exp), reads=[], writes=[tS[0]])
                fw.op("act", lambda: nc.scalar.activation(out=Sc[:, 0:2], in_=Sc[:, 0:2], func=AF.Ln, bias=one1[:, 0:1]), reads=[tone], writes=[tS[0]])
                fw.op("dve", lambda: nc.vector.tensor_tensor(out=Sc[:, 2:4], in0=Sc[:, 0:2], in1=negA[:], op=ALU.mult), reads=[tS[0], tnA], writes=[tS[1]])
                fw.op("pe", lambda: nc.tensor.matmul(banks[4][:, 0:2], tri, Sc[:, 2:4], start=True, stop=True), reads=[tC, tS[1]], writes=[tB[4]])
                fw.op("pe", lambda: nc.tensor.matmul(banks[4][:, 8:10], onesF, Sc[:, 2:4], start=True, stop=True), reads=[tC, tS[1]], acc=[tB[4]])
                for h in range(2):
                    fw.op("pool", lambda: nc.gpsimd.tensor_scalar(out=labc[jc][:, h, :], in0=onesF, scalar1=Sc[:, 2 + h:3 + h], scalar2=None, op0=ALU.mult),
                          reads=[tS[1], tC], writes=[tlabc[jc]])
                for h in range(2):
                    fw.op("pe", lambda: nc.tensor.matmul(banks[5][:, h * 128:(h + 1) * 128], labc[jc][:, h, :], tri, start=True, stop=False),
                          reads=[tlabc[jc], tC], **({"writes": [tB[5]]} if h == 0 else {"acc": [tB[5]]}))
                    fw.op("pe", lambda: nc.tensor.matmul(banks[5][:, h * 128:(h + 1) * 128], identF, posm[:], start=False, stop=True),
                          reads=[tposm, tC], acc=[tB[5]])
                fw.op("pe", lambda: nc.tensor.matmul(banks[5][:, 256:384], BTb[:, cs], CTb[:, cs], start=True, stop=True),
                      reads=[tBTb, tCTb], acc=[tB[5]])
                fw.op("dve", lambda: nc.vector.tensor_copy(out=Sc[:, 12:14], in_=banks[4][:, 0:2]), reads=[tB[4]], writes=[tS[6]])
                fw.op("dve", lambda: nc.vector.tensor_scalar(out=Sc[:, 4:6], in0=banks[4][:, 8:10], scalar1=-1.0, scalar2=None, op0=ALU.mult),
                      reads=[tB[4]], writes=[tS[2]])
                for h in range(2):
                    fw.op("act", lambda: nc.scalar.activation(out=Sc[:, 6 + h:7 + h], in_=banks[4][:, h:h + 1], func=AF.Exp, bias=Sc[:, 4 + h:5 + h]),
                          reads=[tB[4], tS[2]], writes=[tS[3]])
                fw.op("act", lambda: nc.scalar.activation(out=Sc[:, 10:12], in_=banks[4][:, 8:10], func=AF.Exp, scale=-1.0), reads=[tB[4]], writes=[tS[5]])
                fw.op("dve", lambda: nc.vector.tensor_tensor(out=Sc[:, 8:10], in0=Sc[:, 0:2], in1=Sc[:, 6:8], op=ALU.mult), reads=[tS[0], tS[3]], writes=[tS[4]])
                for h in range(2):
                    fw.op("act", lambda: nc.scalar.activation(out=Lr[jc][:, h, :], in_=banks[5][:, h * 128:(h + 1) * 128], func=AF.Exp, scale=-1.0,
                                                              bias=Sc[:, 12 + h:13 + h]),
                          reads=[tB[5], tS[6]], writes=[tLr[jc]])
                fw.op("act", lambda: nc.scalar.activation(out=Sc[:, 14:16], in_=banks[4][:, 0:2], func=AF.Exp, scale=-1.0), reads=[tB[4]], writes=[tS[7]])
                for h in range(2):
                    fw.op("dve", lambda: nc.vector.tensor_tensor(out=WT[jc][:, h, :], in0=Lr[jc][:, h, :], in1=banks[5][:, 256:384], op=ALU.mult),
                          reads=[tLr[jc], tB[5]], writes=[tWT[jc]])
                fw.op("pe", lambda: nc.tensor.transpose(banks[3][:, 0:128], xsT[:, cs], identF), reads=[txsT, tC], writes=[tB[3]])
                fw.op("pe", lambda: nc.tensor.transpose(banks[3][:, 128:256], BTf[:, cs], identF), reads=[tBTf, tC], acc=[tB[3]])
                fw.op("act", lambda: nc.scalar.activation(out=xtm[jc][:], in_=banks[3][:, 0:128], func=AF.Copy), reads=[tB[3]], writes=[txtm[jc]])
                fw.op("act", lambda: nc.scalar.activation(out=Btm[jc][:], in_=banks[3][:, 128:256], func=AF.Copy), reads=[tB[3]], writes=[tBtm[jc]])
                for h in range(2):
                    hs = slice(h * 64, (h + 1) * 64)
                    fw.op("dve", lambda: nc.vector.tensor_scalar(out=xdt[jc][:, hs], in0=xtm[jc][:, hs], scalar1=Sc[:, h:h + 1], scalar2=None, op0=ALU.mult),
                          reads=[txtm[jc], tS[0]], writes=[txdt[jc]])
                    fw.op("pool", lambda: nc.gpsimd.tensor_scalar(out=xdtd[jc][:, hs], in0=xtm[jc][:, hs], scalar1=Sc[:, 8 + h:9 + h], scalar2=None, op0=ALU.mult),
                          reads=[txtm[jc], tS[4]], writes=[txdtd[jc]])
                for h in range(2):
                    hs = slice(h * 64, (h + 1) * 64)
                    fw.op("pe", lambda: nc.tensor.matmul(banks[6][:, hs], WT[jc][:, h, :], xdt[jc][:, hs], start=True, stop=True),
                          reads=[tWT[jc], txdt[jc]], **({"writes": [tB[6]]} if h == 0 else {"acc": [tB[6]]}))
                fw.op("pe", lambda: nc.tensor.matmul(banks[7][:, 128:256], CTb[:, cs], Hb[:], start=True, stop=True), reads=[tCTb, tHb], writes=[tB[7]])
                for h in range(2):
                    hs = slice(h * 64, (h + 1) * 64)
                    fw.op("act", lambda: nc.scalar.activation(out=yo[jc][:, hs], in_=banks[7][:, 128 + h * 64:128 + (h + 1) * 64], func=AF.Copy, scale=Sc[:, 14 + h:15 + h]),
                          reads=[tB[7], tS[7]], writes=[tyo[jc]])
                fw.op("pe", lambda: nc.tensor.matmul(banks[7][:, 0:128], Btm[jc][:], xdtd[jc][:], start=True, stop=True),
                      reads=[tBtm[jc], txdtd[jc]], acc=[tB[7]])
                for h in range(2):
                    hs = slice(h * 64, (h + 1) * 64)
                    fw.op("dve", lambda: nc.vector.scalar_tensor_tensor(out=H[:, hs], in0=H[:, hs], scalar=Sc[:, 10 + h:11 + h], in1=banks[7][:, hs],
                                                                        op0=ALU.mult, op1=ALU.add),
                          reads=[tB[7], tS[5]], writes=[tH])
                fw.op("act", lambda: nc.scalar.activation(out=Hb[:], in_=H[:], func=AF.Copy), reads=[tH], writes=[tHb])
                for h in range(2):
                    hs = slice(h * 64, (h + 1) * 64)
                    fw.op("dve", lambda: nc.vector.scalar_tensor_tensor(out=yy[jc][:, hs], in0=xtm[jc][:, hs], scalar=V[:, VO_DSK + h:VO_DSK + h + 1],
                                                                        in1=banks[6][:, hs], op0=ALU.mult, op1=ALU.add),
                          reads=[tB[6], txtm[jc], tV], writes=[tyy[jc]])
                fw.op("pool", lambda: nc.gpsimd.tensor_tensor(out=yy[jc][:], in0=yy[jc][:], in1=yo[jc][:], op=ALU.add),
                      reads=[tyo[jc]], writes=[tyy[jc]])
                fw.op("pool", lambda: nc.gpsimd.tensor_tensor(out=hout[j][:, cc, :], in0=yy[jc][:], in1=zs[jc][:], op=ALU.mult),
                      reads=[tyy[jc], tzs[jc]], writes=[thout[j]])
            fw.dma("sp", hss_v[:, ti * 4:(ti + 1) * 4, :], hout[j][:], reads=[thout[j]])
        fw.finish("sp")
        pass
    return nc


from concourse.bass_utils import run_bass_kernel_spmd

SEQ = 16384
BATCH = 2
NT_CORE = 4096
TW_T = 256
POOL_W = (2, 4, 8, 16)


def _consts():
    C = np.zeros((128, NCON), np.float32)
    s = np.arange(128)[:, None]
    l = np.arange(128)[None, :]
    C[:, C_TRI:C_TRI + 128] = (s <= l)
    C[:, C_ONE:C_ONE + 128] = 1
    C[:, C_ID:C_ID + 128] = np.eye(128)
    C[64, C_SEL:C_SEL + 64] = 1
    t = np.arange(512)[None, :]
    for j in range(4):
        C[:, C_NM + 512 * j:C_NM + 512 * (j + 1)] = np.where(128 * j + s > t, NEG, 0.0)
    return C


def _e_inputs(xTb, w_in, b_in, mlnorm, g, consts):
    r = lambda a, n: list(range(a, a + n))
    idx = np.array(r(128 * g, 128) + r(512 + 128 * g, 128) + r(1024 + 128 * g, 128) + r(1536 + 128 * g, 128) + r(3080 + 128 * g, 128)
                   + r(2056 + 128 * g, 128) + r(2568 + 128 * g, 128) + [2048 + g, 2052 + g, 3592 + 2 * g, 3593 + 2 * g])
    ws = np.ascontiguousarray(w_in[:, idx])
    bs = b_in[idx]
    bias_bc = np.ascontiguousarray(np.tile(np.concatenate([bs[128:640], bs[896:900]])[None, :], (128, 1)).astype(np.float32))
    bcol = np.ascontiguousarray(np.stack([bs[0:128], bs[128:256], bs[640:768], bs[768:896]], axis=1).astype(np.float32))
    gml = np.ascontiguousarray(np.tile(mlnorm[128 * g:128 * (g + 1)][None, :], (128, 1)).astype(np.float32))
    return {"xT": xTb, "w_sub": ws, "bias_bc": bias_bc, "bcol": bcol, "gml": gml, "consts": consts}


def _o_inputs(xTb, P, j, consts):
    g = j // 2
    r = np.arange(128)
    idx = np.concatenate([512 + 128 * j + r, 1024 + 128 * g + r, 1280 + 128 * g + r, 1544 + 128 * j + r, 128 * j + r, [1536 + 2 * j, 1537 + 2 * j]])
    ws = np.ascontiguousarray(P['w_in'][:, idx])
    V = np.zeros((128, NVO), np.float32)
    chs = [128 * j + r, 512 + 128 * g + r, 768 + 128 * g + r]
    for gi, ch in enumerate(chs):
        for k in range(4):
            V[:, VO_CW + gi * 4 + k] = P['conv_w'][k, ch]
        V[:, VO_CB + gi] = P['conv_b'][ch]
    for h in range(2):
        V[:, VO_DTB + h] = P['dt_bias'][2 * j + h]
        V[:, VO_ALOG + h] = P['a_log'][2 * j + h]
        V[:, VO_DSK + h] = P['d_skip'][2 * j + h]
    V[:, VO_PB] = P['pool_b'][128 * j + r]
    V[:, VO_PS] = P['pool_scale'][128 * j + r]
    w = POOL_W[j]
    V[:, VO_SELW + j] = 1.0 / w
    V[:, VO_CORR:VO_CORR + 16] = (w / np.minimum(np.arange(16) + 1, w))[None, :]
    return {"xT": xTb, "w_sub": ws, "vecs": V, "wgrp": np.ascontiguousarray(P['pool_w'][j]), "consts": consts}


def _t_vecs(ln1g, ln1b, ln2g, ln2b, cw, cb, hmask, sn):
    V = np.zeros((128, NV), np.float32)
    V[:, V_LN1G:V_LN1G + 8] = ln1g.reshape(8, 128).T
    V[:, V_LN1B:V_LN1B + 8] = ln1b.reshape(8, 128).T
    V[:, V_LN2G:V_LN2G + 8] = ln2g.reshape(8, 128).T
    V[:, V_LN2B:V_LN2B + 8] = ln2b.reshape(8, 128).T
    V[:, V_CW:V_CW + 132] = cw.reshape(3, 44, 128).transpose(2, 0, 1).reshape(128, 132)
    V[:, V_CB:V_CB + 44] = cb.reshape(44, 128).T
    V[:, V_HM] = hmask
    if sn is not None:
        V[:, V_SN:V_SN + 4] = sn.reshape(4, 128).T
    return V


def _halo_slice(aT, q):
    out = np.zeros((aT.shape[0], NT_CORE + 2), np.float32)
    lo = q * NT_CORE - 2
    if lo < 0:
        out[:, 2:] = aT[:, 0:NT_CORE]
    else:
        out[:] = aT[:, lo:lo + NT_CORE + 2]
    return out


def kernel(x, ev_w_in, ev_b_in, ev_ml_norm, ev_w_out, od_w_in, od_conv_w, od_conv_b, od_dt_bias, od_a_log, od_d_skip,
           od_ssm_norm, od_pool_w, od_pool_b, od_pool_scale, od_w_out, ffn_w_up, ffn_conv_w, ffn_conv_b, ffn_w_down,
           ln1_g, ln1_b, ln2_g, ln2_b):
    A = lambda a: np.asarray(a, dtype=np.float32)
    x = A(x)
    consts = _consts()
    cores = list(range(8))
    xT = [np.ascontiguousarray(x[b].T) for b in range(BATCH)]
    for layer in range(4):
        jl = layer // 2
        odd = layer % 2
        hcT = [np.zeros((1024, SEQ), np.float32) for _ in range(BATCH)]
        if not odd:
            nc = build_E(SEQ)
            ims = [_e_inputs(xT[c // 4], A(ev_w_in[jl]), A(ev_b_in[jl]), A(ev_ml_norm[jl]), c % 4, consts) for c in cores]
            res = run_bass_kernel_spmd(nc, ims, core_ids=cores).results
            for c in cores:
                b, g = c // 4, c % 4
                hcT[b][128 * g:128 * (g + 1), :] = res[c]["hml"].T
                hcT[b][512 + 128 * g:512 + 128 * (g + 1), :] = res[c]["hfx"]
            w_out = A(ev_w_out[jl])
            sn = None
        else:
            P = {'w_in': A(od_w_in[jl]), 'conv_w': A(od_conv_w[jl]), 'conv_b': A(od_conv_b[jl]), 'dt_bias': A(od_dt_bias[jl]),
                 'a_log': A(od_a_log[jl]), 'd_skip': A(od_d_skip[jl]), 'pool_w': A(od_pool_w[jl]), 'pool_b': A(od_pool_b[jl]),
                 'pool_scale': A(od_pool_scale[jl])}
            nc = build_O(SEQ)
            ims = [_o_inputs(xT[c // 4], P, c % 4, consts) for c in cores]
            res = run_bass_kernel_spmd(nc, ims, core_ids=cores).results
            for c in cores:
                b, j = c // 4, c % 4
                hcT[b][128 * j:128 * (j + 1), :] = res[c]["hss"].T
                hcT[b][512 + 128 * j:512 + 128 * (j + 1), :] = res[c]["hpl"]
            w_out = A(od_w_out[jl])
            sn = A(od_ssm_norm[jl])
        del res
        nc = build_T(NT_CORE, TW_T, odd)
        ims = []
        for c in cores:
            b, q = c // 4, c % 4
            ims.append({"hcT": _halo_slice(hcT[b], q), "xT": _halo_slice(xT[b], q), "w_out": w_out,
                        "w_up": A(ffn_w_up[layer]), "w_down": A(ffn_w_down[layer]),
                        "vecs": _t_vecs(A(ln1_g[layer]), A(ln1_b[layer]), A(ln2_g[layer]), A(ln2_b[layer]), A(ffn_conv_w[layer]),
                                        A(ffn_conv_b[layer]), 0.0 if q == 0 else 1.0, sn)})
        res = run_bass_kernel_spmd(nc, ims, core_ids=cores).results
        xT = [np.zeros((1024, SEQ), np.float32) for _ in range(BATCH)]
        for c in cores:
            b, q = c // 4, c % 4
            xT[b][:, q * NT_CORE:(q + 1) * NT_CORE] = res[c]["x2T"]
        del res
    out = np.stack([np.ascontiguousarray(xT[b].T) for b in range(BATCH)], axis=0)
    return out.astype(np.float32)
```
